# Optimizing a Trainium2 kernel written in Bass

```python
import math
import jax, jax.numpy as jnp
from jax import lax
import numpy as np

D_MODEL = 1024
BATCH = 8
SEQ = 4096
DEPTH = 2

EXPAND = 2
D_MIX = EXPAND * D_MODEL
N_GROUPS = 4
GROUP_W = D_MIX // N_GROUPS
HEAD_DIM = 64
H_A = GROUP_W // (2 * HEAD_DIM)
DV_A = 2 * HEAD_DIM
H_B = GROUP_W // HEAD_DIM
H_C = GROUP_W // HEAD_DIM
H_D = GROUP_W // HEAD_DIM
RWKV_DECAY_RANK = 32
RWKV_ICLR_RANK = 32
RWKV_SHIFT_W = 3 * GROUP_W + RWKV_DECAY_RANK + RWKV_ICLR_RANK
NUM_BUCKETS = 32
MAX_DISTANCE = 128
Q_BLOCK = 128
NORM_EPS = 1e-6
RWKV_LN_EPS = 64e-5
NEG_INF = -1e30
IN_WIDTHS = (GROUP_W, GROUP_W, GROUP_W, GROUP_W,
             GROUP_W, GROUP_W, GROUP_W, GROUP_W,
             GROUP_W, GROUP_W, GROUP_W, GROUP_W, H_C,
             RWKV_SHIFT_W, GROUP_W)
N_IN = sum(IN_WIDTHS)

kernel_name = "hymba_style_diff_sb_fox_rwkv7_hybrid"


def _split(a, widths):
    outs, off = [], 0
    for w in widths:
        outs.append(a[..., off:off + w])
        off += w
    return outs


def rms_norm(x, gain, eps=NORM_EPS):
    xf = x.astype(jnp.float32)
    y = xf * lax.rsqrt(jnp.mean(xf * xf, axis=-1, keepdims=True) + eps)
    return (y * gain.astype(jnp.float32)).astype(x.dtype)


def t5_causal_bucket(dist):
    max_exact = NUM_BUCKETS // 2
    d = jnp.maximum(dist, 1).astype(jnp.float32)
    large = max_exact + (jnp.log(d / max_exact) / math.log(MAX_DISTANCE / max_exact)
                         * (NUM_BUCKETS - max_exact)).astype(jnp.int32)
    large = jnp.minimum(large, NUM_BUCKETS - 1)
    return jnp.where(dist < max_exact, dist, large)


def _blocks(a):
    b, h, s = a.shape[:3]
    a = a.reshape((b, h, s // Q_BLOCK, Q_BLOCK) + a.shape[3:])
    return jnp.moveaxis(a, 2, 0)


def _unblocks(o):
    nb, b, h, qb, dv = o.shape
    return jnp.moveaxis(o, 0, 2).reshape(b, h, nb * qb, dv)


def differential_attention(q, k, v, bias_by_dist, lam):
    seq = q.shape[2]
    pos = jnp.arange(seq)
    scale = q.shape[-1] ** -0.5

    def block(args):
        qb, tb = args
        dist = tb[:, None] - pos[None, :]
        bias = jnp.transpose(bias_by_dist[jnp.clip(dist, 0, seq - 1)], (2, 0, 1)).astype(jnp.float32)
        s = jnp.einsum('bhqcd,bhkcd->bhcqk', qb, k).astype(jnp.float32) * scale + bias[None, :, None]
        s = jnp.where(dist >= 0, s, NEG_INF)
        p = jax.nn.softmax(s, axis=-1)
        w = p[:, :, 0] - lam * p[:, :, 1]
        return jnp.einsum('bhqk,bhkd->bhqd', w.astype(v.dtype), v)

    return _unblocks(lax.map(block, (_blocks(q), pos.reshape(-1, Q_BLOCK))))


def stick_breaking_attention(q, k, v):
    seq = q.shape[2]
    pos = jnp.arange(seq)
    scale = q.shape[-1] ** -0.5

    def block(args):
        qb, tb = args
        z = jnp.einsum('bhqd,bhkd->bhqk', qb, k).astype(jnp.float32) * scale
        causal = tb[:, None] > pos[None, :]
        log_beta = jax.nn.log_sigmoid(z)
        log_1mb = jnp.where(causal, log_beta - z, 0.0)
        after = lax.cumsum(log_1mb, axis=z.ndim - 1, reverse=True) - log_1mb
        a = jnp.where(causal, jnp.exp(log_beta + after), 0.0)
        return jnp.einsum('bhqk,bhkd->bhqd', a.astype(v.dtype), v)

    return _unblocks(lax.map(block, (_blocks(q), pos.reshape(-1, Q_BLOCK))))


def forgetting_attention(q, k, v, log_f):
    seq = q.shape[2]
    pos = jnp.arange(seq)
    scale = q.shape[-1] ** -0.5
    cum_f = jnp.cumsum(log_f, axis=-1)

    def block(args):
        qb, fb, tb = args
        s = jnp.einsum('bhqd,bhkd->bhqk', qb, k).astype(jnp.float32) * scale
        s = s + (fb[..., :, None] - cum_f[..., None, :])
        s = jnp.where(tb[:, None] >= pos[None, :], s, NEG_INF)
        p = jax.nn.softmax(s, axis=-1)
        return jnp.einsum('bhqk,bhkd->bhqd', p.astype(v.dtype), v)

    return _unblocks(lax.map(block, (_blocks(q), _blocks(cum_f), pos.reshape(-1, Q_BLOCK))))


def rwkv7_scan(r, decay, k, v, kk, b):
    bsz, _, h, n = r.shape

    def step(state, inp):
        r_t, w_t, k_t, v_t, kk_t, b_t = inp
        sa = jnp.einsum('bhij,bhj->bhi', state, -kk_t)
        state = (state * w_t[:, :, None, :] + sa[..., None] * b_t[:, :, None, :]
                 + v_t[..., None] * k_t[:, :, None, :])
        return state, jnp.einsum('bhij,bhj->bhi', state, r_t)

    xs = tuple(jnp.moveaxis(u, 1, 0) for u in (r, decay, k, v, kk, b))
    _, y = lax.scan(step, jnp.zeros((bsz, h, n, n), jnp.float32), xs)
    return jnp.moveaxis(y, 0, 1)


def rwkv7_branch(u, mu, w_up, w0, a_up, a0, kkr, ln_gain, ln_bias):
    bsz, seq, _ = u.shape
    f32 = jnp.float32
    u_prev = jnp.pad(u, ((0, 0), (1, 0), (0, 0)))[:, :seq]
    u = u + (u_prev - u) * mu
    r, k, v, w_lo, a_lo = _split(u, (GROUP_W, GROUP_W, GROUP_W, RWKV_DECAY_RANK, RWKV_ICLR_RANK))
    w_log = -jax.nn.softplus(-(w0 + jnp.tanh(w_lo) @ w_up).astype(f32)) - 0.5
    decay = jnp.exp(-jnp.exp(w_log))
    a = jax.nn.sigmoid((a0 + a_lo @ a_up).astype(f32))
    k_k, k_a, r_k = kkr[0].astype(f32), kkr[1].astype(f32), kkr[2].astype(f32)
    r, k, v = r.astype(f32), k.astype(f32), v.astype(f32)
    heads = lambda t: t.reshape(bsz, seq, H_D, HEAD_DIM)
    kk = heads(k * k_k)
    kk = kk / jnp.maximum(jnp.sqrt(jnp.sum(kk * kk, axis=-1, keepdims=True)), 1e-12)
    k = k * (1.0 + (a - 1.0) * k_a)
    rh, kh, vh, ah = heads(r), heads(k), heads(v), heads(a)
    y = rwkv7_scan(rh, heads(decay), kh, vh, kk, kk * ah)
    mean = jnp.mean(y, axis=-1, keepdims=True)
    var = jnp.mean(jnp.square(y - mean), axis=-1, keepdims=True)
    y = ((y - mean) * lax.rsqrt(var + RWKV_LN_EPS) * ln_gain.astype(f32).reshape(H_D, HEAD_DIM)
         + ln_bias.astype(f32).reshape(H_D, HEAD_DIM))
    y = y + jnp.sum(rh * kh * r_k.reshape(H_D, HEAD_DIM), axis=-1, keepdims=True) * vh
    return y.reshape(bsz, seq, GROUP_W).astype(u.dtype)


def setup_inputs(seed: int = 0) -> dict:
    key = jax.random.key(seed)
    ks = jax.random.split(key, 17)
    f32 = jnp.float32
    nrm = lambda k_, shape: jax.random.normal(k_, shape, f32)
    decay_ramp = jnp.tile(jnp.linspace(-6.0, -1.0, HEAD_DIM, dtype=f32), H_D)
    return {
        "x": nrm(ks[0], (BATCH, SEQ, D_MODEL)),
        "norm_gain": 1.0 + 0.02 * nrm(ks[1], (DEPTH, D_MODEL)),
        "w_in": nrm(ks[2], (DEPTH, D_MODEL, N_IN)) * D_MODEL ** -0.5,
        "w_out": nrm(ks[3], (DEPTH, D_MIX, D_MODEL)) * D_MIX ** -0.5,
        "rel_bias": 0.1 * nrm(ks[4], (NUM_BUCKETS, H_A)),
        "qk_gain": 1.0 + 0.02 * nrm(ks[5], (DEPTH, 4, HEAD_DIM)),
        "diff_lambda": 0.1 * nrm(ks[6], (DEPTH, 4, HEAD_DIM)),
        "forget_bias": 3.0 + 0.5 * nrm(ks[7], (DEPTH, H_C)),
        "out_gain": 1.0 + 0.02 * nrm(ks[8], (DEPTH, D_MIX)),
        "rwkv_mu": jax.random.uniform(ks[9], (DEPTH, RWKV_SHIFT_W), f32),
        "rwkv_w_up": 0.5 * nrm(ks[10], (DEPTH, RWKV_DECAY_RANK, GROUP_W)) * RWKV_DECAY_RANK ** -0.5,
        "rwkv_w0": decay_ramp[None, :] + 0.1 * nrm(ks[11], (DEPTH, GROUP_W)),
        "rwkv_a_up": 0.5 * nrm(ks[12], (DEPTH, RWKV_ICLR_RANK, GROUP_W)) * RWKV_ICLR_RANK ** -0.5,
        "rwkv_a0": 0.1 * nrm(ks[13], (DEPTH, GROUP_W)),
        "rwkv_kkr": jnp.array([0.85, 1.0, -0.04], f32)[None, :, None] + 0.02 * nrm(ks[14], (DEPTH, 3, GROUP_W)),
        "rwkv_ln_gain": 1.0 + 0.02 * nrm(ks[15], (DEPTH, GROUP_W)),
        "rwkv_ln_bias": 0.02 * nrm(ks[16], (DEPTH, GROUP_W)),
    }


def reference(x, norm_gain, w_in, w_out, rel_bias, qk_gain, diff_lambda, forget_bias, out_gain,
              rwkv_mu, rwkv_w_up, rwkv_w0, rwkv_a_up, rwkv_a0, rwkv_kkr, rwkv_ln_gain, rwkv_ln_bias):
    bsz, seq, _ = x.shape
    f32 = jnp.float32
    bias_by_dist = rel_bias[t5_causal_bucket(jnp.arange(seq))]
    to_heads = lambda t, h, d: t.reshape(bsz, seq, h, d).transpose(0, 2, 1, 3)
    from_heads = lambda t: t.transpose(0, 2, 1, 3)
    h_res = x
    for l in range(DEPTH):
        h = rms_norm(h_res, norm_gain[l])
        proj = h @ w_in[l]
        (aq, ak, av, ag, bq, bk, bv, bg, cq, ck, cv, cg, cf, d_in, dg) = _split(proj, IN_WIDTHS)

        qa = rms_norm(aq.reshape(bsz, seq, H_A, 2, HEAD_DIM), qk_gain[l, 0]).transpose(0, 2, 1, 3, 4)
        ka = rms_norm(ak.reshape(bsz, seq, H_A, 2, HEAD_DIM), qk_gain[l, 1]).transpose(0, 2, 1, 3, 4)
        va = to_heads(av, H_A, DV_A)
        lam_init = 0.8 - 0.6 * math.exp(-0.3 * l)
        dl = diff_lambda[l].astype(f32)
        lam = jnp.exp(jnp.sum(dl[0] * dl[1])) - jnp.exp(jnp.sum(dl[2] * dl[3])) + lam_init
        oa = from_heads(differential_attention(qa, ka, va, bias_by_dist, lam))
        oa = rms_norm(oa, out_gain[l, :GROUP_W].reshape(H_A, DV_A)) * (1.0 - lam_init)
        oa = oa.reshape(bsz, seq, GROUP_W)

        ob = from_heads(stick_breaking_attention(to_heads(bq, H_B, HEAD_DIM), to_heads(bk, H_B, HEAD_DIM),
                                                 to_heads(bv, H_B, HEAD_DIM)))
        ob = rms_norm(ob, out_gain[l, GROUP_W:2 * GROUP_W].reshape(H_B, HEAD_DIM)).reshape(bsz, seq, GROUP_W)

        qc = from_heads(rms_norm(cq.reshape(bsz, seq, H_C, HEAD_DIM), qk_gain[l, 2]))
        kc = from_heads(rms_norm(ck.reshape(bsz, seq, H_C, HEAD_DIM), qk_gain[l, 3]))
        log_f = jax.nn.log_sigmoid((cf + forget_bias[l]).astype(f32)).transpose(0, 2, 1)
        oc = from_heads(forgetting_attention(qc, kc, to_heads(cv, H_C, HEAD_DIM), log_f))
        oc = rms_norm(oc, out_gain[l, 2 * GROUP_W:3 * GROUP_W].reshape(H_C, HEAD_DIM)).reshape(bsz, seq, GROUP_W)

        od = rwkv7_branch(d_in, rwkv_mu[l], rwkv_w_up[l], rwkv_w0[l], rwkv_a_up[l], rwkv_a0[l],
                          rwkv_kkr[l], rwkv_ln_gain[l] * out_gain[l, 3 * GROUP_W:], rwkv_ln_bias[l])

        mixed = jnp.concatenate([oa * jax.nn.silu(ag), ob * jax.nn.silu(bg),
                                 oc * jax.nn.silu(cg), od * jax.nn.silu(dg)], axis=-1)
        h_res = h_res + mixed @ w_out[l]
    return h_res
```

```python
import contextlib
import math
import numpy as np
import ml_dtypes
import concourse.bass as bass
import concourse.mybir as mybir
from concourse.bass_utils import run_bass_kernel_spmd

F32 = mybir.dt.float32
BF16 = mybir.dt.bfloat16
AF = mybir.ActivationFunctionType
ALU = mybir.AluOpType
AX = mybir.AxisListType

D_MODEL = 1024
N_IN = 8264
D_MIX = 2048
NEG = -30000.0
NORM_EPS = 1e-6
RWKV_LN_EPS = 64e-5


class Dep:
    __slots__ = ("w", "r", "dsem", "dcnt")

    def __init__(self):
        self.w = None
        self.r = {}
        self.dsem = None
        self.dcnt = 0


class Prog:
    SEM_LIMIT = 30000

    def __init__(self, nc, es):
        self.nc = nc
        self.es = es
        self.eng = {"pe": nc.tensor, "act": nc.scalar, "dve": nc.vector, "pool": nc.gpsimd, "sp": nc.sync}
        self.esem = {}
        self.seen = {e: {} for e in self.eng}
        self.nsem = 0
        self.pe_sems = set()
        self.ninst = {e: 0 for e in self.eng}
        self.latest = {}
        self.dma_pool = []
        self.phase_slots = []

    def new_sem(self):
        s = self.es.enter_context(self.nc.semaphore(f"sm{self.nsem}"))
        self.nsem += 1
        return s

    def _bump(self, e):
        cur = self.esem.get(e)
        if cur is None or cur[1] >= self.SEM_LIMIT:
            cur = [self.new_sem(), 0]
            self.esem[e] = cur
            if e == "pe":
                self.pe_sems.add(cur[0])
        cur[1] += 1
        return cur[0], cur[1]

    def _collect(self, e, reads, writes):
        need = {}
        for d in reads:
            t = d.w
            if t is not None and need.get(t[0], 0) < t[1]:
                need[t[0]] = t[1]
        for d in writes:
            t = d.w
            if t is not None and need.get(t[0], 0) < t[1]:
                need[t[0]] = t[1]
            for s, v in d.r.items():
                if need.get(s, 0) < v:
                    need[s] = v
        seen = self.seen[e]
        for s, v in need.items():
            if e == "pe" and s in self.pe_sems:
                continue
            if seen.get(s, 0) >= v:
                continue
            self.eng[e].wait_ge(s, v)
            seen[s] = v

    def _record(self, tok, reads, writes):
        for d in reads:
            if d.r.get(tok[0], 0) < tok[1]:
                d.r[tok[0]] = tok[1]
        for d in writes:
            d.w = tok
            d.r = {}

    def op(self, e, fn, reads=(), writes=()):
        self._collect(e, reads, writes)
        ins = fn(self.eng[e])
        sem, cnt = self._bump(e)
        ins.then_inc(sem, 1)
        self.ninst[e] += 1
        self.latest[sem] = cnt
        self._record((sem, cnt), reads, writes)

    def dma(self, out, in_, reads, writes, slot, q="sp", persistent=False):
        self._collect(q, reads, writes)
        if slot.dsem is None or slot.dsem[1] >= 3500:
            if self.dma_pool:
                slot.dsem = self.dma_pool.pop()
            else:
                slot.dsem = [self.new_sem(), 0]
            if not persistent:
                self.phase_slots.append(slot)
        ent = slot.dsem
        ins = self.eng[q].dma_start(out=out, in_=in_)
        ent[1] += 1
        ins.then_inc(ent[0], 16)
        self.ninst[q] += 1
        self.latest[ent[0]] = 16 * ent[1]
        self._record((ent[0], 16 * ent[1]), reads, writes)

    def barrier(self):
        for e in self.eng:
            seen = self.seen[e]
            for s, v in self.latest.items():
                if seen.get(s, 0) >= v:
                    continue
                self.eng[e].wait_ge(s, v)
                seen[s] = v
        for slot in self.phase_slots:
            if slot.dsem is not None and slot.dsem[1] < 3000:
                self.dma_pool.append(slot.dsem)
            slot.dsem = None
        self.phase_slots = []

    def wait_all(self, e, deps):
        self._collect(e, deps, deps)


class T:
    def __init__(self, t):
        self.t = t
        self.d = Dep()

    def __getitem__(self, k):
        return self.t[k]


PT_NG = 0
PT_QKG = 8
PT_OG = 12
PT_FB = 28
PT_DL = 32
PT_MU = 288
PT_W0 = 301
PT_A0 = 305
PT_KK = 309
PT_KA = 313
PT_RK = 317
PT_LNG = 321
PT_LNB = 325
NPT = 336

C_IDENT = 0
C_ONES = 128
C_BLK64 = 256
C_MINCL = 384
C_MSTRICT = 512
C_NUINCL = 640
C_SU = 768
C_IU = 832
C_SL = 896
NCONST = 960

C32_IDENT = 0
C32_ONES = 128
NC32 = 256


def make_consts():
    c = np.zeros((128, NCONST), np.float32)
    p = np.arange(128)[:, None]
    j = np.arange(128)[None, :]
    c[:, C_IDENT:C_IDENT + 128] = np.eye(128)
    c[:, C_ONES:C_ONES + 128] = 1.0
    c[:, C_BLK64:C_BLK64 + 128] = ((p // 64) == (j // 64))
    c[:, C_MINCL:C_MINCL + 128] = (p <= j)
    c[:, C_MSTRICT:C_MSTRICT + 128] = (p < j)
    c[:, C_NUINCL:C_NUINCL + 128] = -1.0 * (p >= j)
    c[:, C_SU:C_SU + 64] = (p < j[:, :64])
    c[:, C_IU:C_IU + 64] = (p <= j[:, :64])
    c[:, C_SL:C_SL + 64] = (p > j[:, :64])
    return c.astype(ml_dtypes.bfloat16)


def make_consts32():
    c = np.zeros((128, NC32), np.float32)
    c[:, C32_IDENT:C32_IDENT + 128] = np.eye(128)
    c[:, C32_ONES:C32_ONES + 128] = 1.0
    return c


def t5_bucket_np(dist):
    d = np.maximum(dist, 1).astype(np.float32)
    large = 16 + (np.log(d / np.float32(16)) / np.float32(math.log(128 / 16)) * 16).astype(np.int32)
    large = np.minimum(large, 31)
    return np.where(dist < 16, dist, large)


def make_ptab(inp, depth):
    pt = np.zeros((depth, 128, NPT), np.float32)
    p = np.arange(128)
    for l in range(depth):
        pt[l, :, PT_NG:PT_NG + 8] = inp["norm_gain"][l].reshape(8, 128).T
        for i in range(4):
            pt[l, :, PT_QKG + i] = inp["qk_gain"][l, i][p % 64]
        pt[l, :, PT_OG:PT_OG + 16] = inp["out_gain"][l].reshape(16, 128).T
        pt[l, :8, PT_FB] = inp["forget_bias"][l]
        pt[l, :, PT_DL:PT_DL + 256] = inp["diff_lambda"][l].reshape(1, 256)
        mu = inp["rwkv_mu"][l]
        pt[l, :, PT_MU:PT_MU + 12] = mu[:1536].reshape(12, 128).T
        pt[l, :64, PT_MU + 12] = mu[1536:1600]
        pt[l, :, PT_W0:PT_W0 + 4] = inp["rwkv_w0"][l].reshape(4, 128).T
        pt[l, :, PT_A0:PT_A0 + 4] = inp["rwkv_a0"][l].reshape(4, 128).T
        pt[l, :, PT_KK:PT_KK + 4] = inp["rwkv_kkr"][l, 0].reshape(4, 128).T
        pt[l, :, PT_KA:PT_KA + 4] = inp["rwkv_kkr"][l, 1].reshape(4, 128).T
        pt[l, :, PT_RK:PT_RK + 4] = inp["rwkv_kkr"][l, 2].reshape(4, 128).T
        pt[l, :, PT_LNG:PT_LNG + 4] = inp["rwkv_ln_gain"][l].reshape(4, 128).T
        pt[l, :, PT_LNB:PT_LNB + 4] = inp["rwkv_ln_bias"][l].reshape(4, 128).T
    return pt


def make_relbias_tables(rel_bias):
    p = np.arange(128)[:, None]
    i = np.arange(1024)[None, :]
    dist = i - 384 - p
    bucket = t5_bucket_np(np.maximum(dist, 0))
    tbl = np.empty((4, 128, 1024), np.float32)
    for h in range(4):
        tbl[h] = np.where(dist >= 0, rel_bias[bucket, h], np.float32(NEG))
    far = np.zeros((128, 4), np.float32)
    far[:, :] = rel_bias[31][None, :]
    return tbl, far


def host_inputs(inp, depth):
    tbl, far = make_relbias_tables(np.asarray(inp["rel_bias"], np.float32))
    return {
        "w_in": np.ascontiguousarray(inp["w_in"][:depth]),
        "w_out": np.ascontiguousarray(inp["w_out"][:depth]),
        "w_up": np.ascontiguousarray(inp["rwkv_w_up"][:depth]),
        "a_up": np.ascontiguousarray(inp["rwkv_a_up"][:depth]),
        "ptab": make_ptab(inp, depth),
        "consts": make_consts(),
        "consts32": make_consts32(),
        "relb": tbl,
        "relfar": far,
    }


class Rot:
    def __init__(self, items):
        self.items = items
        self.i = 0

    def next(self):
        it = self.items[self.i % len(self.items)]
        self.i += 1
        return it


class _Stop(Exception):
    pass


def build(S=4096, depth=2, debug=None, groups="ABCD", do_out=True, dstop=0):
    nc = bass.Bass("TRN2", target_bir_lowering=False)
    es = contextlib.ExitStack()
    P = Prog(nc, es)
    NTB = S // 512
    NTT = S // 128

    def dram(name, shape, dt=F32, kind="ExternalInput"):
        return nc.dram_tensor(name, list(shape), dt, kind=kind).ap()

    uniq = [0]

    def sbx(stack, name, shape, dt=F32):
        uniq[0] += 1
        return T(stack.enter_context(nc.sbuf_tensor(f"{name}_{uniq[0]}", list(shape), dt)))

    def sb(name, shape, dt=F32):
        return sbx(es, name, shape, dt)

    x_in = dram("x", [S, D_MODEL])
    w_in = dram("w_in", [depth, D_MODEL, N_IN])
    w_out = dram("w_out", [depth, D_MIX, D_MODEL])
    w_up = dram("w_up", [depth, 32, 512])
    a_up = dram("a_up", [depth, 32, 512])
    ptab = dram("ptab", [depth, 128, NPT])
    consts = dram("consts", [128, NCONST], BF16)
    consts32 = dram("consts32", [128, NC32])
    relb = dram("relb", [4, 128, 1024])
    relfar = dram("relfar", [128, 4])
    out = dram("out", [S, D_MODEL], kind="ExternalOutput")
    mixed = dram("mixed", [16, 128, S], BF16, kind="Internal")
    mixed_d = [Dep() for _ in range(16)]
    dbg = None
    if debug == "mixed":
        dbg = dram("dbg", [16, 128, S], BF16, kind="ExternalOutput")

    cst = sb("cst", [128, NCONST], BF16)
    P.dma(cst[:], consts[:, :], [], [cst.d], cst.d, persistent=True)
    c32 = sb("c32", [128, NC32])
    P.dma(c32[:], consts32[:, :], [], [c32.d], c32.d, persistent=True)
    rfar = sb("rfar", [128, 4])
    P.dma(rfar[:], relfar[:, :], [], [rfar.d], rfar.d, persistent=True)
    ident = cst.t[:, C_IDENT:C_IDENT + 128]
    ones_b = cst.t[:, C_ONES:C_ONES + 128]
    blk64 = cst.t[:, C_BLK64:C_BLK64 + 128]
    m_incl = cst.t[:, C_MINCL:C_MINCL + 128]
    m_strict = cst.t[:, C_MSTRICT:C_MSTRICT + 128]
    nuincl = cst.t[:, C_NUINCL:C_NUINCL + 128]
    pt = [sb(f"ptab{l}", [128, NPT + 16]) for l in range(depth)]
    PT_X = NPT
    for l in range(depth):
        P.dma(pt[l].t[:, 0:NPT], ptab[l], [], [pt[l].d], pt[l].d, persistent=True)

    ps = [T(es.enter_context(nc.psum_tensor(f"ps{b}", [128, 512], F32))) for b in range(8)]
    misc_banks = Rot([ps[0], ps[1]])
    proj_banks = Rot([ps[0], ps[1], ps[4], ps[5], ps[6], ps[7]])

    hT = sb("hT", [128, 8, S], BF16)
    hT_d = [[Dep() for _ in range(NTB)] for _ in range(8)]

    dbgstate = []
    lam_init = [0.8 - 0.6 * math.exp(-0.3 * l) for l in range(depth)]

    def layer_prep(l):
        p = pt[l]
        d = p.d
        P.op("dve", lambda e: e.tensor_scalar(out=p.t[:, PT_X:PT_X + 1], in0=p.t[:, PT_QKG:PT_QKG + 1], scalar1=0.125,
                                              scalar2=None, op0=ALU.mult), [d], [d])
        P.op("dve", lambda e: e.tensor_scalar(out=p.t[:, PT_X + 1:PT_X + 2], in0=p.t[:, PT_QKG + 2:PT_QKG + 3], scalar1=0.125,
                                              scalar2=None, op0=ALU.mult), [d], [d])
        P.op("dve", lambda e: e.tensor_scalar(out=p.t[:, PT_X + 2:PT_X + 3], in0=p.t[:, PT_FB:PT_FB + 1], scalar1=-1.0,
                                              scalar2=None, op0=ALU.mult), [d], [d])
        tmp = sb(f"lamtmp{l}", [128, 132])
        P.op("dve", lambda e: e.tensor_tensor(out=tmp.t[:, 0:64], in0=p.t[:, PT_DL:PT_DL + 64], in1=p.t[:, PT_DL + 64:PT_DL + 128],
                                              op=ALU.mult), [d], [tmp.d])
        P.op("dve", lambda e: e.tensor_tensor(out=tmp.t[:, 64:128], in0=p.t[:, PT_DL + 128:PT_DL + 192],
                                              in1=p.t[:, PT_DL + 192:PT_DL + 256], op=ALU.mult), [d], [tmp.d])
        P.op("dve", lambda e: e.reduce_sum(out=tmp.t[:, 128:129], in_=tmp.t[:, 0:64], axis=AX.X), [tmp.d], [tmp.d])
        P.op("dve", lambda e: e.reduce_sum(out=tmp.t[:, 129:130], in_=tmp.t[:, 64:128], axis=AX.X), [tmp.d], [tmp.d])
        P.op("act", lambda e: e.activation(out=tmp.t[:, 130:132], in_=tmp.t[:, 128:130], func=AF.Exp), [tmp.d], [tmp.d])
        P.op("dve", lambda e: e.tensor_tensor(out=tmp.t[:, 128:129], in0=tmp.t[:, 131:132], in1=tmp.t[:, 130:131],
                                              op=ALU.subtract), [tmp.d], [tmp.d])
        P.op("dve", lambda e: e.tensor_scalar(out=p.t[:, PT_X + 3:PT_X + 4], in0=tmp.t[:, 128:129], scalar1=-lam_init[l],
                                              scalar2=None, op0=ALU.add), [tmp.d, d], [d])
        P.op("dve", lambda e: e.tensor_tensor(out=p.t[:, PT_X + 4:PT_X + 8], in0=p.t[:, PT_LNG:PT_LNG + 4],
                                              in1=p.t[:, PT_OG + 12:PT_OG + 16], op=ALU.mult), [d], [d])

    def phase_norm(l, src, src_deps):
        with contextlib.ExitStack() as st:
            xs = [sbx(st, f"xs{i}", [128, D_MODEL]) for i in range(2)]
            xn = sbx(st, "xn", [128, 4, D_MODEL], BF16)
            xn_d = [Dep() for _ in range(4)]
            junk = sbx(st, "junk", [128, D_MODEL], BF16)
            ss = sbx(st, "ss", [128, 8])
            ss_d = [Dep() for _ in range(8)]
            for tb in range(NTB):
                for j in range(4):
                    tt = tb * 4 + j
                    sl = xs[tt % 2]
                    P.dma(sl[:], src[tt * 128:(tt + 1) * 128, :], src_deps(tt), [sl.d], sl.d)
                    P.op("act", lambda e, sl=sl, j=j: e.activation(out=junk[:], in_=sl[:], func=AF.Square,
                                                                   accum_out=ss[:, j:j + 1]),
                         [sl.d], [junk.d, ss_d[j]])
                    P.op("dve", lambda e, j=j: e.tensor_scalar(out=ss[:, 4 + j:5 + j], in0=ss[:, j:j + 1],
                                                               scalar1=1.0 / D_MODEL, scalar2=NORM_EPS,
                                                               op0=ALU.mult, op1=ALU.add), [ss_d[j]], [ss_d[4 + j]])
                    P.op("act", lambda e, j=j: e.activation(out=ss[:, 4 + j:5 + j], in_=ss[:, 4 + j:5 + j], func=AF.Sqrt),
                         [ss_d[4 + j]], [ss_d[4 + j]])
                    P.op("dve", lambda e, j=j: e.reciprocal(out=ss[:, 4 + j:5 + j], in_=ss[:, 4 + j:5 + j]),
                         [ss_d[4 + j]], [ss_d[4 + j]])
                    P.op("dve", lambda e, sl=sl, j=j: e.tensor_scalar(out=xn[:, j, :], in0=sl[:], scalar1=ss[:, 4 + j:5 + j],
                                                                      scalar2=None, op0=ALU.mult),
                         [sl.d, ss_d[4 + j]], [xn_d[j]])
                for c in range(8):
                    bank = misc_banks.next()
                    pb = bank.t[:].bitcast(BF16)

                    def tr(e, c=c, pb=pb):
                        r = None
                        for j in range(4):
                            r = e.transpose(out=pb[:, j * 128:(j + 1) * 128], in_=xn[:, j, c * 128:(c + 1) * 128],
                                            identity=ident)
                        return r
                    P.op("pe", tr, xn_d + [cst.d], [bank.d])
                    P.op("act", lambda e, c=c, pb=pb, tb=tb: e.activation(out=hT[:, c, tb * 512:(tb + 1) * 512],
                                                                          in_=pb[:, 0:512], func=AF.Copy,
                                                                          scale=pt[l][:, PT_NG + c:PT_NG + c + 1]),
                         [bank.d, pt[l].d], [hT_d[c][tb]])
            P.barrier()

    wstack = contextlib.ExitStack()
    wst = [sbx(es, f"wst{i}", [128, 8, 128]) for i in range(2)]
    wbf = [sbx(es, f"wbf{i}", [128, 8, 128], BF16) for i in range(2)]
    wrot = [0]

    cast_eng = ["pool"]

    def load_w(l, col0, M):
        i = wrot[0] % 2
        wrot[0] += 1
        src = w_in[l].rearrange("(c p) n -> p c n", p=128)[:, :, col0:col0 + M]
        P.dma(wst[i].t[:, :, 0:M], src, [], [wst[i].d], wst[i].d, persistent=True)
        if cast_eng[0] == "pool":
            P.op("pool", lambda e: e.tensor_copy(out=wbf[i].t[:, :, 0:M], in_=wst[i].t[:, :, 0:M]), [wst[i].d], [wbf[i].d])
        else:
            P.op("act", lambda e: e.activation(out=wbf[i].t[:, :, 0:M], in_=wst[i].t[:, :, 0:M], func=AF.Copy), [wst[i].d], [wbf[i].d])
        return wbf[i]

    def inproj_T(l, col0, M, evac, banks=None):
        wb = load_w(l, col0, M)
        for tb in range(NTB):
            bank = (banks or proj_banks).next()

            def mm(e, bank=bank, tb=tb):
                r = None
                for c in range(8):
                    r = e.matmul(bank.t[0:M, :], lhsT=wb.t[:, c, 0:M], rhs=hT[:, c, tb * 512:(tb + 1) * 512],
                                 start=(c == 0), stop=(c == 7))
                return r
            P.op("pe", mm, [wb.d] + [hT_d[c][tb] for c in range(8)], [bank.d])
            evac(tb, bank)

    def inproj_tok(l, col0, N, evac):
        wb = load_w(l, col0, N)
        for tt in range(NTT):
            bank = proj_banks.next()

            def mm(e, bank=bank, tt=tt):
                r = None
                for c in range(8):
                    r = e.matmul(bank.t[:, 0:N], lhsT=hT[:, c, tt * 128:(tt + 1) * 128], rhs=wb.t[:, c, 0:N],
                                 start=(c == 0), stop=(c == 7))
                return r
            P.op("pe", mm, [wb.d] + [hT_d[c][tt // 4] for c in range(8)], [bank.d])
            evac(tt, bank)

    def phase_attn(l, grps):
        with contextlib.ExitStack() as st:
            QT = sbx(st, "QT", [128, S], BF16)
            KT = sbx(st, "KT", [128, S], BF16)
            GT = sbx(st, "GT", [128, S], BF16)
            VT = sbx(st, "VT", [128, NTT, 128], BF16)
            MT = sbx(st, "MT", [128, S], BF16)
            QT_d = [Dep() for _ in range(NTB)]
            KT_d = [Dep() for _ in range(NTB)]
            GT_d = [Dep() for _ in range(NTB)]
            VT_d = [Dep() for _ in range(NTT)]
            MT_d = [Dep() for _ in range(NTB)]
            f32t = Rot([sbx(st, f"f32t{i}", [128, 512]) for i in range(8)])
            b16t = Rot([sbx(st, f"b16t{i}", [128, 512], BF16) for i in range(10)])
            Rt = sbx(st, "Rt", [128, 512])
            Rt2 = sbx(st, "Rt2", [128, 512])
            tbl = sbx(st, "tbl", [128, 1024])
            biasC = sbx(st, "biasC", [128, 2, NTB, NTT])
            Gk = sbx(st, "Gk", [128, NTT, 8])
            Gmid = sbx(st, "Gmid", [128, 8 * NTB])
            zbanks = Rot([ps[2], ps[3]])

            def evac_qk_norm(dst, dst_d, gaincol):
                def ev(tb, bank):
                    sq = b16t.next()
                    P.op("act", lambda e: e.activation(out=sq[:], in_=bank[:], func=AF.Square), [bank.d], [sq.d])
                    b2 = zbanks.next()
                    P.op("pe", lambda e: e.matmul(b2[:], lhsT=blk64, rhs=sq[:], start=True, stop=True), [sq.d, cst.d], [b2.d])
                    rs = f32t.next()
                    P.op("act", lambda e: e.activation(out=rs[:], in_=b2[:], func=AF.Ln, scale=1.0 / 64, bias=eps_t[:, 0:1]),
                         [b2.d, eps_t.d], [rs.d])
                    P.op("act", lambda e: e.activation(out=rs[:], in_=rs[:], func=AF.Exp, scale=-0.5), [rs.d], [rs.d])
                    P.op("dve", lambda e: e.scalar_tensor_tensor(out=dst[:, tb * 512:(tb + 1) * 512], in0=bank[:], scalar=gaincol,
                                                                 in1=rs[:], op0=ALU.mult, op1=ALU.mult),
                         [bank.d, rs.d, pt[l].d], [dst_d[tb]])
                return ev

            def evac_scale(dst, dst_d, scale):
                def ev(tb, bank):
                    P.op("act", lambda e: e.activation(out=dst[:, tb * 512:(tb + 1) * 512], in_=bank[:], func=AF.Copy, scale=scale),
                         [bank.d], [dst_d[tb]])
                return ev

            def evac_copy_dve(dst, dst_d):
                def ev(tb, bank):
                    P.op("dve", lambda e: e.tensor_copy(out=dst[:, tb * 512:(tb + 1) * 512], in_=bank[:]), [bank.d], [dst_d[tb]])
                return ev

            def evac_silu(tb, bank):
                P.op("act", lambda e: e.activation(out=GT[:, tb * 512:(tb + 1) * 512], in_=bank[:], func=AF.Silu),
                     [bank.d], [GT_d[tb]])

            def evac_v(tt, bank):
                eng = "dve" if tt % 2 else "act"
                if eng == "dve":
                    P.op("dve", lambda e: e.tensor_copy(out=VT[:, tt, :], in_=bank[:, 0:128]), [bank.d], [VT_d[tt]])
                else:
                    P.op("act", lambda e: e.activation(out=VT[:, tt, :], in_=bank[:, 0:128], func=AF.Copy), [bank.d], [VT_d[tt]])

            eps_t = sbx(st, "eps_t", [128, 4])
            P.op("dve", lambda e: e.memset(eps_t[:, 0:1], NORM_EPS), [], [eps_t.d])
            P.op("dve", lambda e: e.memset(eps_t[:, 2:3], 1.0), [eps_t.d], [eps_t.d])

            def fox_prep():
                cfw = sbx(st, "cfw", [8, 512])
                cfl = sbx(st, "cfl", [8, 512])
                Gc = sbx(st, "Gc", [8, 512])
                ones8 = sbx(st, "ones8", [8, 512])
                carry = sbx(st, "carry", [8, NTB + 1])
                gdiag = sbx(st, "gdiag", [8, 8, NTB])
                P.op("dve", lambda e: e.memset(ones8[:], 1.0), [], [ones8.d])
                P.op("dve", lambda e: e.memset(carry[:], 0.0), [], [carry.d])
                gkbank = ps[4]

                def ev(tb, bank):
                    P.op("act", lambda e: e.activation(out=cfw[:], in_=bank[0:8, :], func=AF.Exp, scale=-1.0,
                                                       bias=pt[l].t[0:8, PT_X + 2:PT_X + 3]), [bank.d, pt[l].d], [cfw.d])
                    P.op("act", lambda e: e.activation(out=cfl[:], in_=cfw[:], func=AF.Ln, scale=1.0, bias=ones8[:, 0:1]),
                         [cfw.d, ones8.d], [cfl.d])
                    P.op("dve", lambda e: e.tensor_tensor_scan(out=Gc[:], data0=ones8[:], data1=cfl[:],
                                                               initial=carry[:, tb:tb + 1], op0=ALU.mult, op1=ALU.add),
                         [ones8.d, cfl.d, carry.d], [Gc.d])
                    P.op("dve", lambda e: e.tensor_copy(out=carry[:, tb + 1:tb + 2], in_=Gc[:, 511:512]), [Gc.d], [carry.d])
                    P.op("dve", lambda e: e.tensor_scalar(out=gdiag[:, :, tb], in0=c32.t[0:8, C32_IDENT:C32_IDENT + 8],
                                                          scalar1=Gc[:, 256:257], scalar2=None, op0=ALU.mult),
                         [Gc.d, c32.d], [gdiag.d])

                    def tr(e):
                        r = None
                        for j in range(4):
                            tt = tb * 4 + j
                            r = e.transpose(out=gkbank.t[:, tt * 8:(tt + 1) * 8], in_=Gc[:, j * 128:(j + 1) * 128],
                                            identity=c32.t[0:8, C32_IDENT:C32_IDENT + 8])
                        return r
                    P.op("pe", tr, [Gc.d, c32.d], [gkbank.d])
                inproj_T(l, 6144, 8, ev, banks=misc_banks)
                P.op("dve", lambda e: e.tensor_copy(out=Gk[:].rearrange("p a b -> p (a b)"), in_=gkbank.t[:, 0:NTT * 8]),
                     [gkbank.d], [Gk.d])
                b2 = ps[5]
                P.op("pe", lambda e: e.matmul(b2.t[:, 0:8 * NTB], lhsT=c32.t[0:8, C32_ONES:C32_ONES + 128],
                                              rhs=gdiag[:].rearrange("p a b -> p (a b)"), start=True, stop=True),
                     [gdiag.d, c32.d], [b2.d])
                P.op("dve", lambda e: e.tensor_copy(out=Gmid[:], in_=b2.t[:, 0:8 * NTB]), [b2.d], [Gmid.d])

            def rms_gate_store(osb, qt, M_ones, inv_n, c2, gaincol):
                if debug == "osb" and not dbgstate:
                    dbgstate.append(1)
                    dd = dram("dbg_osb", [128, 512], F32, kind="ExternalOutput")
                    P.dma(dd, osb[:], [osb.d], [], osb.d)
                    for i_, t_ in enumerate(b16t.items):
                        dd = dram(f"dbg_b16_{i_}", [128, 512], BF16, kind="ExternalOutput")
                        P.dma(dd, t_[:], [t_.d], [], t_.d)
                sq = b16t.next()
                P.op("act", lambda e: e.activation(out=sq[:], in_=osb[:], func=AF.Square), [osb.d], [sq.d])
                b2 = misc_banks.next()
                P.op("pe", lambda e: e.matmul(b2[:], lhsT=M_ones, rhs=sq[:], start=True, stop=True), [sq.d, cst.d], [b2.d])
                rs = f32t.next()
                P.op("act", lambda e: e.activation(out=rs[:], in_=b2[:], func=AF.Ln, scale=inv_n / c2,
                                                   bias=eps_t[:, 1:2]), [b2.d, eps_t.d], [rs.d])
                P.op("act", lambda e: e.activation(out=rs[:], in_=rs[:], func=AF.Exp, scale=-0.5), [rs.d], [rs.d])
                P.op("dve", lambda e: e.tensor_tensor(out=rs[:], in0=rs[:], in1=osb[:], op=ALU.mult), [rs.d, osb.d], [rs.d])
                P.op("dve", lambda e: e.scalar_tensor_tensor(out=MT[:, qt * 512:(qt + 1) * 512], in0=rs[:], scalar=gaincol,
                                                             in1=GT[:, qt * 512:(qt + 1) * 512], op0=ALU.mult, op1=ALU.mult),
                     [rs.d, GT_d[qt], pt[l].d], [MT_d[qt]])

            def ktiles(qt):
                res = []
                for kt in range(4 * qt + 4):
                    c0 = 128 * (kt - 4 * qt) if kt >= 4 * qt else 0
                    res.append((kt, c0))
                return res

            for grp in grps:
                base = {"A": 0, "B": 2048, "C": 4096}[grp]
                if grp == "C":
                    fox_prep()
                c2 = 1.0
                if grp == "A":
                    c2 = (1.0 - lam_init[l]) ** 2
                P.op("dve", lambda e, c2=c2: e.memset(eps_t[:, 1:2], NORM_EPS / c2), [eps_t.d], [eps_t.d])
                for u in range(4):
                    mt = {"A": 0, "B": 4, "C": 8}[grp] + u
                    qcol, kcol, vcol, gcol = base + u * 128, base + 512 + u * 128, base + 1024 + u * 128, base + 1536 + u * 128
                    if grp == "A":
                        inproj_T(l, qcol, 128, evac_qk_norm(QT, QT_d, pt[l].t[:, PT_X:PT_X + 1]))
                        inproj_T(l, kcol, 128, evac_qk_norm(KT, KT_d, pt[l].t[:, PT_QKG + 1:PT_QKG + 2]))
                        P.dma(tbl[:], relb[u], [], [tbl.d], tbl.d)
                    elif grp == "C":
                        inproj_T(l, qcol, 128, evac_qk_norm(QT, QT_d, pt[l].t[:, PT_X + 1:PT_X + 2]))
                        inproj_T(l, kcol, 128, evac_qk_norm(KT, KT_d, pt[l].t[:, PT_QKG + 3:PT_QKG + 4]))
                        for hh in range(2):
                            h = 2 * u + hh
                            for qt in range(NTB):
                                P.op("dve", lambda e, hh=hh, h=h, qt=qt: e.tensor_scalar(
                                    out=biasC[:, hh, qt, :], in0=Gk[:, :, h], scalar1=Gmid[:, h * NTB + qt:h * NTB + qt + 1],
                                    scalar2=None, op0=ALU.subtract), [Gk.d, Gmid.d], [biasC.d])
                    else:
                        inproj_T(l, qcol, 128, evac_scale(QT, QT_d, 0.125))
                        inproj_T(l, kcol, 128, evac_copy_dve(KT, KT_d))
                    inproj_T(l, gcol, 128, evac_silu)
                    inproj_tok(l, vcol, 128, evac_v)
                    if debug == "unit":
                        P.barrier()
                        for nm, tt_, shp, dt_ in [("QT", QT, [128, S], BF16), ("KT", KT, [128, S], BF16), ("GT", GT, [128, S], BF16),
                                                 ("VT", VT, [128, NTT, 128], BF16), ("biasC", biasC, [128, 2, NTB, NTT], F32),
                                                 ("Gk", Gk, [128, NTT, 8], F32), ("Gmid", Gmid, [128, 8 * NTB], F32)]:
                            dd = dram("dbg_" + nm, shp, dt_, kind="ExternalOutput")
                            P.dma(dd, tt_.t[:], [], [], tt_.d)
                        P.barrier()
                        return

                    Rts = [Rt, Rt2]
                    for qt in range(NTB):
                        qs = qt * 512
                        kts = ktiles(qt)
                        if grp in ("A", "C"):
                            zr = [Rot([ps[2], ps[3]]), Rot([ps[0], ps[1]])]
                            if grp == "A":
                                subs = [dict(b0=64 * c, vsl=slice(0, 128), orow=slice(0, 128), ot=ps[4 + 2 * c], den=ps[5 + 2 * c], hh=c)
                                        for c in range(2)]
                            else:
                                pair = (ps[4], ps[5]) if qt % 2 == 0 else (ps[6], ps[7])
                                subs = [dict(b0=64 * hh, vsl=slice(64 * hh, 64 * hh + 64), orow=slice(64 * hh, 64 * hh + 64),
                                             ot=pair[0], den=pair[1], hh=hh) for hh in range(2)]
                            n = len(kts)
                            pend = {}
                            LAG = 2
                            for idx in range(n + LAG):
                                if idx < n:
                                    kt, c0 = kts[idx]
                                    zbs = []
                                    for si, sub in enumerate(subs):
                                        b0 = sub["b0"]
                                        zb = zr[si].next()
                                        zbs.append(zb)
                                        P.op("pe", lambda e, zb=zb, kt=kt, c0=c0, b0=b0: e.matmul(
                                            zb.t[:, c0:512], lhsT=KT[b0:b0 + 64, kt * 128:(kt + 1) * 128],
                                            rhs=QT[b0:b0 + 64, qs + c0:qs + 512], start=True, stop=True),
                                            [KT_d[kt // 4], QT_d[qt]], [zb.d])
                                    for si, sub in enumerate(subs):
                                        zb = zbs[si]
                                        ptile = b16t.next()
                                        if grp == "A":
                                            near = kt >= 4 * qt - 1
                                            if near:
                                                off = qs - kt * 128 + 384
                                                tmp = f32t.next()
                                                P.op("dve", lambda e, tmp=tmp, zb=zb, c0=c0, off=off: e.tensor_tensor(
                                                    out=tmp.t[:, c0:512], in0=zb.t[:, c0:512], in1=tbl.t[:, off + c0:off + 512],
                                                    op=ALU.add), [zb.d, tbl.d], [tmp.d])
                                                P.op("act", lambda e, tmp=tmp, ptile=ptile, c0=c0: e.activation(
                                                    out=ptile.t[:, c0:512], in_=tmp.t[:, c0:512], func=AF.Exp), [tmp.d], [ptile.d])
                                            else:
                                                P.op("act", lambda e, zb=zb, ptile=ptile, c0=c0: e.activation(
                                                    out=ptile.t[:, c0:512], in_=zb.t[:, c0:512], func=AF.Exp,
                                                    bias=rfar.t[:, u:u + 1]), [zb.d, rfar.d], [ptile.d])
                                        else:
                                            hh = sub["hh"]
                                            P.op("act", lambda e, zb=zb, ptile=ptile, c0=c0, hh=hh, kt=kt: e.activation(
                                                out=ptile.t[:, c0:512], in_=zb.t[:, c0:512], func=AF.Exp,
                                                bias=biasC[:, hh, qt, kt:kt + 1]), [zb.d, biasC.d], [ptile.d])
                                            if kt >= 4 * qt:
                                                P.op("pool", lambda e, ptile=ptile, c0=c0: e.tensor_tensor(
                                                    out=ptile.t[:, c0:c0 + 128], in0=ptile.t[:, c0:c0 + 128], in1=m_incl,
                                                    op=ALU.mult), [ptile.d, cst.d], [ptile.d])
                                        pend[(idx, si)] = (kt, c0, ptile)
                                if idx >= LAG:
                                    for si, sub in enumerate(subs):
                                        kt, c0, ptile = pend.pop((idx - LAG, si))

                                        def pv(e, kt=kt, c0=c0, ptile=ptile, sub=sub):
                                            e.matmul(sub["ot"].t[sub["orow"], c0:512], lhsT=VT[:, kt, sub["vsl"]],
                                                     rhs=ptile.t[:, c0:512], start=(kt == 0), stop=(kt == 4 * qt + 3),
                                                     skip_group_check=True)
                                            M = sub["orow"].stop - sub["orow"].start
                                            return e.matmul(sub["den"].t[sub["orow"], c0:512], lhsT=ones_b[:, 0:M],
                                                            rhs=ptile.t[:, c0:512], start=(kt == 0), stop=(kt == 4 * qt + 3),
                                                            skip_group_check=True)
                                        P.op("pe", pv, [VT_d[kt], ptile.d, cst.d], [sub["ot"].d, sub["den"].d])
                            fin_bank = Rot([ps[1], ps[0]])
                            if grp == "A":
                                r0 = f32t.next()
                                r1 = f32t.next()
                                o0 = f32t.next()
                                o1 = f32t.next()
                                P.op("act", lambda e, r0=r0: e.activation(out=r0[:], in_=ps[5][:], func=AF.Ln), [ps[5].d], [r0.d])
                                P.op("dve", lambda e, o0=o0: e.tensor_copy(out=o0[:], in_=ps[4][:]), [ps[4].d], [o0.d])
                                P.op("act", lambda e, r1=r1: e.activation(out=r1[:], in_=ps[7][:], func=AF.Ln), [ps[7].d], [r1.d])
                                P.op("dve", lambda e, o1=o1: e.tensor_copy(out=o1[:], in_=ps[6][:]), [ps[6].d], [o1.d])
                                P.op("act", lambda e, r0=r0: e.activation(out=r0[:], in_=r0[:], func=AF.Exp, scale=-1.0), [r0.d], [r0.d])
                                P.op("act", lambda e, r1=r1: e.activation(out=r1[:], in_=r1[:], func=AF.Exp, scale=-1.0), [r1.d], [r1.d])
                                P.op("pool", lambda e, r0=r0, o0=o0: e.tensor_tensor(out=r0[:], in0=o0[:], in1=r0[:], op=ALU.mult),
                                     [o0.d, r0.d], [r0.d])
                                P.op("pool", lambda e, r1=r1, o1=o1: e.tensor_tensor(out=r1[:], in0=o1[:], in1=r1[:], op=ALU.mult),
                                     [o1.d, r1.d], [r1.d])
                                osb = f32t.next()
                                P.op("dve", lambda e, r0=r0, r1=r1, osb=osb: e.scalar_tensor_tensor(
                                    out=osb[:], in0=r1[:], scalar=pt[l].t[:, PT_X + 3:PT_X + 4], in1=r0[:], op0=ALU.mult, op1=ALU.add),
                                    [r0.d, r1.d, pt[l].d], [osb.d])
                                rms_gate_store(osb, qt, ones_b, 1.0 / 128, c2, pt[l].t[:, PT_OG + mt:PT_OG + mt + 1])
                            else:
                                ot, den = subs[0]["ot"], subs[0]["den"]
                                r0 = f32t.next()
                                P.op("dve", lambda e, r0=r0, den=den: e.reciprocal(out=r0[:], in_=den[:]), [den.d], [r0.d])
                                osb = f32t.next()
                                P.op("dve", lambda e, r0=r0, ot=ot, osb=osb: e.tensor_tensor(out=osb[:], in0=ot[:], in1=r0[:], op=ALU.mult),
                                     [ot.d, r0.d], [osb.d])
                                rms_gate_store(osb, qt, blk64, 1.0 / 64, 1.0, pt[l].t[:, PT_OG + mt:PT_OG + mt + 1])
                        else:
                            ot = ps[7]
                            zr = [Rot([ps[2], ps[3]]), Rot([ps[4], ps[5]])]
                            csbs = [ps[6], ps[0]]
                            kts_r = kts[::-1]
                            n = len(kts_r)
                            for hh in range(2):
                                P.op("pool", lambda e, hh=hh: e.memset(Rts[hh][:], 0.0), [], [Rts[hh].d])
                            st1 = {}
                            st2 = {}
                            for idx in range(n + 2):
                                if idx < n:
                                    kt, c0 = kts_r[idx]
                                    zbs = []
                                    for hh in range(2):
                                        b0 = 64 * hh
                                        zb = zr[hh].next()
                                        zbs.append(zb)
                                        P.op("pe", lambda e, zb=zb, kt=kt, c0=c0, b0=b0: e.matmul(
                                            zb.t[:, c0:512], lhsT=KT[b0:b0 + 64, kt * 128:(kt + 1) * 128],
                                            rhs=QT[b0:b0 + 64, qs + c0:qs + 512], start=True, stop=True, skip_group_check=True),
                                            [KT_d[kt // 4], QT_d[qt]], [zb.d])
                                    for hh in range(2):
                                        zb = zbs[hh]
                                        ee = f32t.next()
                                        P.op("act", lambda e, zb=zb, ee=ee, c0=c0: e.activation(
                                            out=ee.t[:, c0:512], in_=zb.t[:, c0:512], func=AF.Exp), [zb.d], [ee.d])
                                        sp = b16t.next()
                                        P.op("act", lambda e, sp=sp, ee=ee, c0=c0: e.activation(
                                            out=sp.t[:, c0:512], in_=ee.t[:, c0:512], func=AF.Ln, scale=1.0, bias=eps_t[:, 2:3]),
                                            [ee.d, eps_t.d], [sp.d])
                                        if kt >= 4 * qt:
                                            P.op("pool", lambda e, sp=sp, c0=c0: e.tensor_tensor(
                                                out=sp.t[:, c0:c0 + 128], in0=sp.t[:, c0:c0 + 128], in1=m_strict, op=ALU.mult),
                                                [sp.d, cst.d], [sp.d])
                                        st1[(idx, hh)] = (kt, c0, sp, zb)
                                if 1 <= idx <= n:
                                    for hh in range(2):
                                        kt, c0, sp, db = st1.pop((idx - 1, hh))
                                        csb = csbs[hh]
                                        Rh = Rts[hh]

                                        def dmm(e, db=db, kt=kt, c0=c0, sp=sp, csb=csb):
                                            e.matmul(db.t[:, c0:512], lhsT=nuincl, rhs=sp.t[:, c0:512], start=False, stop=True,
                                                     skip_group_check=True)
                                            return e.matmul(csb.t[:, c0:512], lhsT=ones_b, rhs=sp.t[:, c0:512], start=True, stop=True)
                                        P.op("pe", dmm, [sp.d, cst.d], [db.d, csb.d])
                                        tmp = f32t.next()
                                        P.op("dve", lambda e, tmp=tmp, db=db, c0=c0, Rh=Rh: e.tensor_tensor(
                                            out=tmp.t[:, c0:512], in0=db.t[:, c0:512], in1=Rh.t[:, c0:512], op=ALU.subtract),
                                            [db.d, Rh.d], [tmp.d])
                                        P.op("dve", lambda e, c0=c0, Rh=Rh, csb=csb: e.tensor_tensor(
                                            out=Rh.t[:, c0:512], in0=csb.t[:, c0:512], in1=Rh.t[:, c0:512], op=ALU.add),
                                            [csb.d, Rh.d], [Rh.d])
                                        at = b16t.next()
                                        P.op("act", lambda e, at=at, tmp=tmp, c0=c0: e.activation(
                                            out=at.t[:, c0:512], in_=tmp.t[:, c0:512], func=AF.Exp), [tmp.d], [at.d])
                                        if kt >= 4 * qt:
                                            P.op("pool", lambda e, at=at, c0=c0: e.tensor_tensor(
                                                out=at.t[:, c0:c0 + 128], in0=at.t[:, c0:c0 + 128], in1=m_strict, op=ALU.mult),
                                                [at.d, cst.d], [at.d])
                                        st2[(idx, hh)] = (kt, c0, at)
                                if idx >= 2:
                                    for hh in range(2):
                                        orow = slice(64 * hh, 64 * hh + 64)
                                        kt, c0, at = st2.pop((idx - 1, hh))
                                        P.op("pe", lambda e, kt=kt, c0=c0, at=at, orow=orow, first=(idx == 2), last=(idx == n + 1): e.matmul(
                                            ot.t[orow, c0:512], lhsT=VT[:, kt, orow], rhs=at.t[:, c0:512],
                                            start=first, stop=last, skip_group_check=True), [VT_d[kt], at.d], [ot.d])
                            osb = f32t.next()
                            P.op("act", lambda e, osb=osb: e.activation(out=osb[:], in_=ot[:], func=AF.Copy), [ot.d], [osb.d])
                            rms_gate_store(osb, qt, blk64, 1.0 / 64, 1.0, pt[l].t[:, PT_OG + mt:PT_OG + mt + 1])
                    P.dma(mixed[mt], MT[:], MT_d, [mixed_d[mt]], MT.d)
            P.barrier()

    def phase_rwkv(l):
        cast_eng[0] = "act"
        p = pt[l]
        su = cst.t[0:64, C_SU:C_SU + 64]
        iu = cst.t[0:64, C_IU:C_IU + 64]
        slm = cst.t[0:64, C_SL:C_SL + 64]
        id64 = cst.t[0:64, C_IDENT:C_IDENT + 64]
        with contextlib.ExitStack() as st:
            zb_ = Rot([ps[2], ps[3]])
            LW = sbx(st, "LW", [64, 512], BF16)
            rmask = sbx(st, "rmask", [128, 512])
            P.op("pool", lambda e: e.memset(rmask[:], 1.0), [], [rmask.d])
            P.op("pool", lambda e: e.memset(rmask.t[:].rearrange("p (c j) -> p c j", j=64)[:, :, 0:1], 0.0), [rmask.d], [rmask.d])
            carry = sbx(st, "carry", [128, 16])
            P.op("pool", lambda e: e.memset(carry[:], 0.0), [], [carry.d])
            MS = sbx(st, "MS", [64, 8, 64])
            MSbs = [sbx(st, f"MSb{i}", [64, 8, 64], BF16) for i in range(2)]
            P.op("pool", lambda e: e.memset(MS[:], 0.0), [], [MS.d])
            for MSb_ in MSbs:
                P.op("pool", lambda e, MSb_=MSb_: e.memset(MSb_[:], 0.0), [], [MSb_.d])
            epsd = sbx(st, "epsd", [128, 2])
            P.op("pool", lambda e: e.memset(epsd[:, 0:1], RWKV_LN_EPS), [], [epsd.d])
            ARt = sbx(st, "ARt", [128, 8, 128], BF16)
            AR = [ARt for c in range(4)]
            ARl = [sbx(st, f"ARl{c}", [64, 2, 8, 128], BF16) for c in range(4)]
            BKl = sbx(st, "BKl", [64, 2, 8, 128], BF16)
            PCl = sbx(st, "PCl", [64, 8, 8])
            Vt = [sbx(st, f"Vt{c}", [64, 8, 128], BF16) for c in range(4)]
            Bh = [sbx(st, f"Bh{c}", [64, 8, 128], BF16) for c in range(4)]
            Kh = [sbx(st, f"Kh{c}", [64, 8, 128], BF16) for c in range(4)]
            ArbT = [sbx(st, f"ArbT{c}", [64, 16, 64], BF16) for c in range(4)]
            AakT = [sbx(st, f"AakT{c}", [64, 16, 64], BF16) for c in range(4)]
            ArkT = [sbx(st, f"ArkT{c}", [64, 16, 64], BF16) for c in range(4)]
            TTm = [sbx(st, f"TTm{c}", [64, 16, 64], BF16) for c in range(4)]
            PCv = sbx(st, "PCv", [128, 4, 8])
            gtd = [sbx(st, f"gtd{c}", [128, 512], BF16) for c in range(4)]
            bon = [sbx(st, f"bon{c}", [128, 512], BF16) for c in range(4)]
            BK = sbx(st, "BK", [128, 8, 128], BF16)
            uraw = sbx(st, "uraw", [128, 513])
            rp = sbx(st, "rp", [128, 512])
            kp = sbx(st, "kp", [128, 512])
            vp = sbx(st, "vp", [128, 512])
            lb = sbx(st, "lb", [64, 512], BF16)
            tt_ = [sbx(st, f"dt{i}", [128, 512]) for i in range(7)]

            class _View:
                def __init__(self, base):
                    self.t = base.t[0:64, :].bitcast(BF16).rearrange("p (a b) -> p a b", b=64)
                    self.d = base.d

                def __getitem__(self, k):
                    return self.t[k]
            Yp = [_View(tt_[0]), _View(tt_[1])]
            Zp = [_View(tt_[2]), _View(tt_[3])]
            P.dma(tt_[4].t[0:32, :], w_up[l], [], [tt_[4].d], tt_[4].d)
            P.dma(tt_[4].t[32:64, :], a_up[l], [], [tt_[4].d], tt_[4].d)
            P.op("pool", lambda e: e.tensor_copy(out=LW[:], in_=tt_[4].t[0:64, :]), [tt_[4].d], [LW.d])
            bt_ = Rot([sbx(st, f"db{i}", [128, 512], BF16) for i in range(3)])
            R1 = sbx(st, "R1", [64, 512], BF16)
            Ub = sbx(st, "Ub", [64, 512], BF16)
            MTd = sbx(st, "MTd", [128, 512], BF16)
            ysb = tt_[2]

            def TTo(eng, out, in0, in1, op, rd, wr):
                P.op(eng, lambda e: e.tensor_tensor(out=out, in0=in0, in1=in1, op=op), rd, wr)

            def TSo(eng, out, in0, s1, s2, op0, op1, rd, wr):
                P.op(eng, lambda e: e.tensor_scalar(out=out, in0=in0, scalar1=s1, scalar2=s2, op0=op0, op1=op1), rd, wr)

            def STTo(out, in0, sc, in1, op0, op1, rd, wr):
                P.op("dve", lambda e: e.scalar_tensor_tensor(out=out, in0=in0, scalar=sc, in1=in1, op0=op0, op1=op1), rd, wr)

            def ACTo(out, in_, func, rd, wr, scale=1.0, bias=None):
                if bias is None:
                    P.op("act", lambda e: e.activation(out=out, in_=in_, func=func, scale=scale), rd, wr)
                else:
                    P.op("act", lambda e: e.activation(out=out, in_=in_, func=func, scale=scale, bias=bias), rd, wr)

            def seg_inproj(col0, M, tb, evac):
                wb = load_w(l, col0, M)
                bank = misc_banks.next()

                def mm(e):
                    r = None
                    for c in range(8):
                        r = e.matmul(bank.t[0:M, :], lhsT=wb.t[:, c, 0:M], rhs=hT[:, c, tb * 512:(tb + 1) * 512],
                                     start=(c == 0), stop=(c == 7))
                    return r
                P.op("pe", mm, [wb.d] + [hT_d[c][tb] for c in range(8)], [bank.d])
                evac(bank)

            def shifted(bank, M, cidx, mucol, dst, dst_dep):
                P.op("act", lambda e: e.activation(out=uraw.t[0:M, 1:513], in_=bank.t[0:M, :], func=AF.Copy), [bank.d], [uraw.d])
                P.op("pool", lambda e: e.tensor_copy(out=uraw.t[0:M, 0:1], in_=carry.t[0:M, cidx:cidx + 1]), [carry.d, uraw.d], [uraw.d])
                P.op("pool", lambda e: e.tensor_copy(out=carry.t[0:M, cidx:cidx + 1], in_=uraw.t[0:M, 512:513]), [uraw.d], [carry.d])
                d_ = tt_[6]
                TTo("dve", d_.t[0:M, :], uraw.t[0:M, 0:512], uraw.t[0:M, 1:513], ALU.subtract, [uraw.d], [d_.d])
                STTo(dst, d_.t[0:M, :], mucol, uraw.t[0:M, 1:513], ALU.mult, ALU.add, [d_.d, uraw.d, p.d], [dst_dep])

            v3 = lambda ap: ap.rearrange("p (c j) -> p c j", j=64)

            def chk(k):
                if dstop == k:
                    raise _Stop()

            def seg_body(tb):
                def lora(tbx):
                    def ev_lora(bank):
                        l32 = tt_[5]
                        shifted(bank, 64, 12, p.t[0:64, PT_MU + 12:PT_MU + 13], l32.t[0:64, :], l32.d)
                        ACTo(lb.t[0:32, :], l32.t[0:32, :], AF.Tanh, [l32.d], [lb.d])
                        P.op("pool", lambda e: e.tensor_copy(out=lb.t[32:64, :], in_=l32.t[32:64, :]), [l32.d, lb.d], [lb.d])
                    seg_inproj(6152 + 1536, 64, tbx, ev_lora)

                def d1(ct, tbx=tb, gate=True, only_gate=False):
                    if only_gate:
                        seg_inproj(7752 + ct * 128, 128, tbx,
                                   lambda bank: ACTo(gtd[ct][:], bank[:], AF.Silu, [bank.d], [gtd[ct].d]))
                        return
                    seg_inproj(6152 + ct * 128, 128, tbx,
                               lambda bank: shifted(bank, 128, ct, p.t[:, PT_MU + ct:PT_MU + ct + 1], rp[:], rp.d))
                    seg_inproj(6152 + 512 + ct * 128, 128, tbx,
                               lambda bank: shifted(bank, 128, 4 + ct, p.t[:, PT_MU + 4 + ct:PT_MU + 5 + ct], kp[:], kp.d))
                    seg_inproj(6152 + 1024 + ct * 128, 128, tbx,
                               lambda bank: shifted(bank, 128, 8 + ct, p.t[:, PT_MU + 8 + ct:PT_MU + 9 + ct], vp[:], vp.d))
                    if gate:
                        seg_inproj(7752 + ct * 128, 128, tbx,
                                   lambda bank: ACTo(gtd[ct][:], bank[:], AF.Silu, [bank.d], [gtd[ct].d]))

                if tb == 0 or dstop:
                    lora(tb)
                    chk(1)
                    d1(0)
                else:
                    d1(0, only_gate=True)

                def d2(ct):
                    chk(2)
                    t1, t2, t3, t4, t5, t6, t7 = tt_
                    b1 = zb_.next()
                    P.op("pe", lambda e: e.matmul(b1[:], lhsT=LW.t[0:32, ct * 128:(ct + 1) * 128], rhs=lb.t[0:32, :], start=True, stop=True),
                         [LW.d, lb.d], [b1.d])
                    ACTo(t1[:], b1[:], AF.Sigmoid, [b1.d, p.d], [t1.d], bias=p.t[:, PT_W0 + ct:PT_W0 + ct + 1])
                    b2 = zb_.next()
                    P.op("pe", lambda e: e.matmul(b2[:], lhsT=LW.t[32:64, ct * 128:(ct + 1) * 128], rhs=lb.t[32:64, :], start=True, stop=True),
                         [LW.d, lb.d], [b2.d])
                    ACTo(t2[:], b2[:], AF.Sigmoid, [b2.d, p.d], [t2.d], bias=p.t[:, PT_A0 + ct:PT_A0 + ct + 1])
                    P.op("dve", lambda e: e.tensor_tensor_scan(out=t3[:], data0=rmask[:], data1=t1[:], initial=0.0,
                                                               op0=ALU.mult, op1=ALU.add), [rmask.d, t1.d], [t3.d])
                    ACTo(PCv.t[:, ct, :], v3(t3[:])[:, :, 63], AF.Exp, [t3.d], [PCv.d], scale=-0.6065306597126334)
                    ACTo(t4[:], t3[:], AF.Exp, [t3.d], [t4.d], scale=-0.6065306597126334)
                    TTo("pool", AR[ct].t[:, :, 64:128], v3(rp[:]), v3(t4[:]), ALU.mult, [rp.d, t4.d], [AR[ct].d])
                    chk(3)
                    TSo("dve", t5[:], kp[:], p.t[:, PT_KK + ct:PT_KK + ct + 1], None, ALU.mult, ALU.bypass, [kp.d, p.d], [t5.d])
                    sq = bt_.next()
                    ACTo(sq[:], t5[:], AF.Square, [t5.d], [sq.d])
                    b3 = zb_.next()
                    P.op("pe", lambda e: e.matmul(b3[:], lhsT=blk64, rhs=sq[:], start=True, stop=True), [sq.d, cst.d], [b3.d])
                    ACTo(t6[:], b3[:], AF.Sqrt, [b3.d], [t6.d])
                    TSo("dve", t6[:], t6[:], 1e-12, None, ALU.max, ALU.bypass, [t6.d], [t6.d])
                    ACTo(t6[:], t6[:], AF.Ln, [t6.d], [t6.d])
                    ACTo(t6[:], t6[:], AF.Exp, [t6.d], [t6.d], scale=-1.0)
                    TTo("pool", t5[:], t5[:], t6[:], ALU.mult, [t5.d, t6.d], [t5.d])
                    TTo("dve", t7[:], t3[:], t1[:], ALU.subtract, [t3.d, t1.d], [t7.d])
                    ACTo(t7[:], t7[:], AF.Exp, [t7.d], [t7.d], scale=-0.6065306597126334)
                    STTo(AR[ct].t[:, :, 0:64], v3(t5[:]), -1.0, v3(t7[:]), ALU.mult, ALU.mult, [t5.d, t7.d], [AR[ct].d])
                    TSo("dve", t6[:], t2[:], 1.0, p.t[:, PT_KA + ct:PT_KA + ct + 1], ALU.subtract, ALU.mult, [t2.d, p.d], [t6.d])
                    STTo(t6[:], t6[:], 1.0, kp[:], ALU.add, ALU.mult, [t6.d, kp.d], [t6.d])
                    TTo("pool", t5[:], t5[:], t2[:], ALU.mult, [t5.d, t2.d], [t5.d])
                    ACTo(t4[:], t3[:], AF.Exp, [t3.d], [t4.d], scale=0.6065306597126334)
                    TTo("pool", BK.t[:, :, 64:128], v3(t6[:]), v3(t4[:]), ALU.mult, [t6.d, t4.d], [BK.d])
                    TTo("dve", BK.t[:, :, 0:64], v3(t5[:]), v3(t4[:]), ALU.mult, [t5.d, t4.d], [BK.d])
                    chk(4)
                    cumC = v3(t3[:])[:, :, 63:64].to_broadcast([128, 8, 64])
                    TTo("dve", v3(t7[:]), cumC, v3(t3[:]), ALU.subtract, [t3.d, t7.d], [t7.d])
                    ACTo(t7[:], t7[:], AF.Exp, [t7.d], [t7.d], scale=-0.6065306597126334)
                    khc = bt_.next()
                    TTo("pool", khc[:], t6[:], t7[:], ALU.mult, [t6.d, t7.d], [khc.d])
                    bhc = bt_.next()
                    TTo("dve", bhc[:], t5[:], t7[:], ALU.mult, [t5.d, t7.d], [bhc.d])
                    vbc = bt_.next()
                    P.op("pool", lambda e: e.tensor_copy(out=vbc[:], in_=vp[:]), [vp.d], [vbc.d])
                    chk(45)
                    for src_, dst_ in ((khc, Kh[ct]), (bhc, Bh[ct]), (vbc, Vt[ct])):
                        bk_ = zb_.next()
                        pb = bk_.t[:].bitcast(BF16)

                        def trs(e, src_=src_, pb=pb):
                            r = None
                            for cc in range(8):
                                r = e.transpose(out=pb[0:64, cc * 128:(cc + 1) * 128], in_=src_.t[:, cc * 64:(cc + 1) * 64],
                                                identity=ident)
                            return r
                        P.op("pe", trs, [src_.d, cst.d], [bk_.d])
                        chk(46)
                        P.op("act", lambda e, dst_=dst_, pb=pb: e.activation(out=dst_.t[:].rearrange("p a b -> p (a b)"),
                                                                             in_=pb[0:64, 0:1024], func=AF.Copy), [bk_.d], [dst_.d])
                        chk(47)
                        if dst_ is Bh[ct]:
                            chk(48)
                    chk(5)
                    prb = bt_.next()
                    STTo(prb[:], rp[:], p.t[:, PT_RK + ct:PT_RK + ct + 1], t6[:], ALU.mult, ALU.mult, [rp.d, t6.d, p.d], [prb.d])
                    b4 = zb_.next()
                    P.op("pe", lambda e: e.matmul(b4[:], lhsT=blk64, rhs=prb[:], start=True, stop=True), [prb.d, cst.d], [b4.d])
                    TTo("dve", bon[ct][:], b4[:], vp[:], ALU.mult, [b4.d, vp.d], [bon[ct].d])

                def d3(ct):
                    chk(6)
                    for hh in range(2):
                        P.dma(ARl[ct].t[:, hh], ARt.t[64 * hh:64 * hh + 64], [ARt.d], [ARl[ct].d], ARl[ct].d)
                        P.dma(BKl.t[:, hh], BK.t[64 * hh:64 * hh + 64], [BK.d], [BKl.d], BKl.d)
                    for grp4 in range(4):
                        bkO = zb_.next()

                        def mmO(e, grp4=grp4, bkO=bkO, lo=0):
                            r = None
                            for i4 in range(4):
                                blk = grp4 * 4 + i4
                                cc, hh = blk // 2, blk % 2
                                r = e.matmul(bkO.t[0:64, i4 * 128:(i4 + 1) * 128], lhsT=BKl.t[:, hh, cc, lo:lo + 64],
                                             rhs=ARl[ct].t[:, hh, cc, :], start=True, stop=True)
                            return r
                        P.op("pe", mmO, [BKl.d, ARl[ct].d], [bkO.d])
                        chk(61)
                        o4 = bkO.t[0:64, :].rearrange("p (b c) -> p b c", c=128)
                        sub = slice(grp4 * 4, grp4 * 4 + 4)
                        TTo("dve", Yp[0].t[:, sub, :], o4[:, :, 0:64], su.unsqueeze(1).to_broadcast([64, 4, 64]), ALU.mult,
                            [bkO.d, cst.d], [Yp[0].d])
                        chk(62)
                        TTo("dve", ArbT[ct].t[:, sub, :], o4[:, :, 64:128], iu.unsqueeze(1).to_broadcast([64, 4, 64]), ALU.mult,
                            [bkO.d, cst.d], [ArbT[ct].d])
                        bkK = zb_.next()
                        P.op("pe", lambda e, grp4=grp4, bkK=bkK: mmO(e, grp4, bkK, 64), [BKl.d, ARl[ct].d], [bkK.d])
                        k4 = bkK.t[0:64, :].rearrange("p (b c) -> p b c", c=128)
                        TTo("dve", AakT[ct].t[:, sub, :], k4[:, :, 0:64], su.unsqueeze(1).to_broadcast([64, 4, 64]), ALU.mult,
                            [bkK.d, cst.d], [AakT[ct].d])
                        TTo("dve", ArkT[ct].t[:, sub, :], k4[:, :, 64:128], iu.unsqueeze(1).to_broadcast([64, 4, 64]), ALU.mult,
                            [bkK.d, cst.d], [ArkT[ct].d])
                    for half in range(2):
                        bkN = zb_.next()

                        def mmN(e, half=half, bkN=bkN):
                            r = None
                            for i8 in range(8):
                                blk = half * 8 + i8
                                cc, hh = blk // 2, blk % 2
                                r = e.matmul(bkN.t[0:64, i8 * 64:(i8 + 1) * 64], lhsT=ARl[ct].t[:, hh, cc, 0:64],
                                             rhs=BKl.t[:, hh, cc, 0:64], start=True, stop=True)
                            return r
                        P.op("pe", mmN, [BKl.d, ARl[ct].d], [bkN.d])
                        TTo("dve", Zp[0].t[:, half * 8:half * 8 + 8, :], bkN.t[0:64, :].rearrange("p (b c) -> p b c", c=64),
                            slm.unsqueeze(1).to_broadcast([64, 8, 64]), ALU.mult, [bkN.d, cst.d], [Zp[0].d])
                    chk(7)
                    Tm = TTm[ct]
                    TTo("pool", Tm[:], Yp[0][:], id64.unsqueeze(1).to_broadcast([64, 16, 64]), ALU.add, [Yp[0].d, cst.d], [Tm.d])
                    cur = 0
                    for step in range(5):
                        Yc, Zc, Yn, Zn = Yp[cur], Zp[cur], Yp[1 - cur], Zp[1 - cur]

                        def mm16(e, bankA, bankB, L, Rr):
                            r = None
                            for blk in range(16):
                                bank = bankA if blk < 8 else bankB
                                r = e.matmul(bank.t[0:64, (blk % 8) * 64:(blk % 8 + 1) * 64], lhsT=L.t[:, blk, :], rhs=Rr.t[:, blk, :],
                                             start=True, stop=True)
                            return r

                        def evac16(eng, dst, bankA, bankB, add=None):
                            for hf, bank in enumerate((bankA, bankB)):
                                src3 = bank.t[0:64, :].rearrange("p (b c) -> p b c", c=64)
                                dsl = dst.t[:, hf * 8:hf * 8 + 8, :]
                                if add is None:
                                    if eng == "act":
                                        P.op("act", lambda e, src3=src3, dsl=dsl: e.activation(out=dsl, in_=src3, func=AF.Copy), [bank.d], [dst.d])
                                    else:
                                        P.op("dve", lambda e, src3=src3, dsl=dsl: e.tensor_copy(out=dsl, in_=src3), [bank.d], [dst.d])
                                else:
                                    TTo("dve", dsl, src3, dsl, ALU.add, [bank.d, dst.d], [dst.d])
                        zA, zB = ps[4], ps[5]
                        P.op("pe", lambda e: mm16(e, zA, zB, Yc, Zc), [Yc.d, Zc.d], [zA.d, zB.d])
                        evac16("act", Zn, zA, zB)
                        if step < 4:
                            yA, yB = ps[6], ps[7]
                            P.op("pe", lambda e: mm16(e, yA, yB, Zc, Yc), [Yc.d, Zc.d], [yA.d, yB.d])
                            evac16("dve", Yn, yA, yB)
                        tA, tB = ps[2], ps[3]
                        P.op("pe", lambda e: mm16(e, tA, tB, Zn, Tm), [Zn.d, Tm.d], [tA.d, tB.d])
                        evac16("dve", Tm, tA, tB, add=True)
                        cur = 1 - cur

                for ct in range(4):
                    d2(ct)
                    if ct < 3:
                        d1(ct + 1)
                    d3(ct)
                chk(8)
                for hh in range(2):
                    P.dma(PCl.t[:].rearrange("p (c h) k -> p c h k", h=2)[:, :, hh, :], PCv.t[64 * hh:64 * hh + 64, :, :],
                          [PCv.d], [PCl.d], PCl.d)
                ybank = [ps[4], ps[5], ps[6], ps[7]]
                for cc in range(8):
                    MSb = MSbs[cc % 2]
                    MSbn = MSbs[(cc + 1) % 2]
                    rb = zb_.next()

                    def mmR(e, cc=cc, rb=rb):
                        r = None
                        for ct in range(4):
                            for hh in range(2):
                                o = rb.t[0:64, (ct * 2 + hh) * 64:(ct * 2 + hh + 1) * 64]
                                e.matmul(o, lhsT=ARl[ct].t[:, hh, cc, 0:64], rhs=MSb.t[:, ct * 2 + hh, :],
                                         start=True, stop=False)
                                r = e.matmul(o, lhsT=AakT[ct].t[:, cc * 2 + hh, :], rhs=Vt[ct].t[0:64, cc, hh * 64:(hh + 1) * 64],
                                             start=False, stop=True)
                        return r
                    P.op("pe", mmR, [MSb.d] + [x_.d for c in range(4) for x_ in (ARl[c], AakT[c], Vt[c])], [rb.d])
                    P.op("act", lambda e, rb=rb: e.activation(out=R1[:], in_=rb.t[0:64, :], func=AF.Copy), [rb.d], [R1.d])
                    ub = zb_.next()

                    def mmU(e, cc=cc, ub=ub):
                        r = None
                        for ct in range(4):
                            for hh in range(2):
                                blk = ct * 2 + hh
                                r = e.matmul(ub.t[0:64, blk * 64:(blk + 1) * 64], lhsT=TTm[ct].t[:, cc * 2 + hh, :],
                                             rhs=R1.t[:, blk * 64:(blk + 1) * 64], start=True, stop=True)
                        return r
                    P.op("pe", mmU, [R1.d] + [TTm[c].d for c in range(4)], [ub.d])
                    P.op("dve", lambda e, ub=ub: e.tensor_copy(out=Ub[:], in_=ub.t[0:64, :]), [ub.d], [Ub.d])
                    mb = zb_.next()

                    def mmM(e, cc=cc, mb=mb):
                        r = None
                        for ct in range(4):
                            for hh in range(2):
                                blk = ct * 2 + hh
                                o2 = mb.t[0:64, blk * 64:(blk + 1) * 64]
                                e.matmul(o2, lhsT=Bh[ct].t[0:64, cc, hh * 64:(hh + 1) * 64], rhs=Ub.t[:, blk * 64:(blk + 1) * 64],
                                         start=True, stop=False, skip_group_check=True)
                                r = e.matmul(o2, lhsT=Kh[ct].t[0:64, cc, hh * 64:(hh + 1) * 64], rhs=Vt[ct].t[0:64, cc, hh * 64:(hh + 1) * 64],
                                             start=False, stop=True, skip_group_check=True)
                        return r
                    P.op("pe", mmM, [Ub.d] + [x_.d for c in range(4) for x_ in (Vt[c], Bh[c], Kh[c])], [mb.d])

                    def mmY(e, cc=cc, MSb=MSb):
                        r = None
                        for ct in range(4):
                            for hh in range(2):
                                blk = ct * 2 + hh
                                o = ybank[ct].t[64 * hh:64 * hh + 64, cc * 64:(cc + 1) * 64]
                                e.matmul(o, lhsT=MSb.t[:, blk, :], rhs=ARl[ct].t[:, hh, cc, 64:128],
                                         start=True, stop=False, skip_group_check=True)
                                e.matmul(o, lhsT=Ub.t[:, blk * 64:(blk + 1) * 64], rhs=ArbT[ct].t[:, cc * 2 + hh, :],
                                         start=False, stop=False, skip_group_check=True)
                                r = e.matmul(o, lhsT=Vt[ct].t[0:64, cc, hh * 64:(hh + 1) * 64], rhs=ArkT[ct].t[:, cc * 2 + hh, :],
                                             start=False, stop=True, skip_group_check=True)
                        return r
                    P.op("pe", mmY, [MSb.d, Ub.d] + [x_.d for c in range(4) for x_ in (ARl[c], ArbT[c], ArkT[c], Vt[c])],
                         [b_.d for b_ in ybank])
                    TTo("dve", MS[:], MS[:], PCl.t[:, :, cc:cc + 1].to_broadcast([64, 8, 64]), ALU.mult, [MS.d, PCl.d], [MS.d])
                    TTo("dve", MS[:], MS[:], mb.t[0:64, :].rearrange("p (a b) -> p a b", b=64), ALU.add, [MS.d, mb.d], [MS.d])
                    P.op("act", lambda e, MSbn=MSbn: e.activation(out=MSbn[:], in_=MS[:], func=AF.Copy), [MS.d], [MSbn.d])
                chk(9)
                if tb + 1 < NTB and not dstop:
                    lora(tb + 1)
                    d1(0, tb + 1, gate=False)
                for ct in range(4):
                    yb = ybank[ct]
                    ybf = bt_.next()
                    P.op("act", lambda e: e.activation(out=ysb[:], in_=yb[:], func=AF.Copy), [yb.d], [ysb.d])
                    P.op("pool", lambda e: e.tensor_copy(out=ybf[:], in_=ysb[:]), [ysb.d], [ybf.d])
                    mbk = zb_.next()
                    P.op("pe", lambda e: e.matmul(mbk[:], lhsT=blk64, rhs=ybf[:], start=True, stop=True), [ybf.d, cst.d], [mbk.d])
                    yc = tt_[0]
                    STTo(yc[:], mbk[:], -1.0 / 64, ysb[:], ALU.mult, ALU.add, [mbk.d, ysb.d], [yc.d])
                    sq = bt_.next()
                    ACTo(sq[:], yc[:], AF.Square, [yc.d], [sq.d])
                    vbk = zb_.next()
                    P.op("pe", lambda e: e.matmul(vbk[:], lhsT=blk64, rhs=sq[:], start=True, stop=True), [sq.d, cst.d], [vbk.d])
                    rs = tt_[1]
                    ACTo(rs[:], vbk[:], AF.Ln, [vbk.d, epsd.d], [rs.d], scale=1.0 / 64, bias=epsd.t[:, 0:1])
                    ACTo(rs[:], rs[:], AF.Exp, [rs.d], [rs.d], scale=-0.5)
                    TTo("dve", yc[:], yc[:], rs[:], ALU.mult, [yc.d, rs.d], [yc.d])
                    TSo("dve", yc[:], yc[:], p.t[:, PT_X + 4 + ct:PT_X + 5 + ct], p.t[:, PT_LNB + ct:PT_LNB + ct + 1], ALU.mult, ALU.add,
                        [yc.d, p.d], [yc.d])
                    TTo("pool", yc[:], yc[:], bon[ct][:], ALU.add, [yc.d, bon[ct].d], [yc.d])
                    TTo("pool", MTd[:], yc[:], gtd[ct][:], ALU.mult, [yc.d, gtd[ct].d], [MTd.d])
                    P.dma(mixed[12 + ct][:, tb * 512:(tb + 1) * 512], MTd[:], [MTd.d], [mixed_d[12 + ct]], MTd.d)

            try:
                for tb in range(NTB):
                    seg_body(tb)
            except _Stop:
                pass
            cast_eng[0] = "pool"
            P.barrier()

    res = dram("res", [S, D_MODEL], kind="Internal")

    def phase_out(l, src, dst):
        with contextlib.ExitStack() as st:
            wo = sbx(st, "wo", [128, 16, 1024], BF16)
            wo_d = [Dep() for _ in range(16)]
            wos = [sbx(st, f"wos{i}", [128, 1024]) for i in range(2)]
            for t in range(16):
                sl = wos[t % 2]
                P.dma(sl[:], w_out[l][t * 128:(t + 1) * 128, :], [], [sl.d], sl.d)
                P.op("pool", lambda e, sl=sl, t=t: e.tensor_copy(out=wo.t[:, t, :], in_=sl[:]), [sl.d], [wo_d[t]])
            mx = [sbx(st, f"mx{i}", [128, 16, 128], BF16) for i in range(2)]
            xr = [sbx(st, f"xr{i}", [128, 1024]) for i in range(2)]
            ot_ = [sbx(st, f"ot{i}", [128, 1024]) for i in range(2)]
            obanks = [(ps[0], ps[1]), (ps[2], ps[3])]
            for tt in range(NTT):
                m_ = mx[tt % 2]
                P.dma(m_[:], mixed[:, :, tt * 128:(tt + 1) * 128].rearrange("t p n -> p t n"), mixed_d, [m_.d], m_.d, q="act")
                x_ = xr[tt % 2]
                P.dma(x_[:], src[tt * 128:(tt + 1) * 128, :], [], [x_.d], x_.d, q="pool")
                o_ = ot_[tt % 2]
                for half in range(2):
                    bank = obanks[tt % 2][half]

                    def mm(e, m_=m_, half=half, bank=bank):
                        r = None
                        for t in range(16):
                            r = e.matmul(bank[:], lhsT=m_.t[:, t, :], rhs=wo.t[:, t, half * 512:(half + 1) * 512],
                                         start=(t == 0), stop=(t == 15))
                        return r
                    P.op("pe", mm, [m_.d] + wo_d, [bank.d])
                    P.op("dve", lambda e, o_=o_, x_=x_, bank=bank, half=half: e.tensor_tensor(
                        out=o_.t[:, half * 512:(half + 1) * 512], in0=bank[:], in1=x_.t[:, half * 512:(half + 1) * 512], op=ALU.add),
                        [bank.d, x_.d], [o_.d])
                P.dma(dst[tt * 128:(tt + 1) * 128, :], o_[:], [o_.d], [], o_.d)
            P.barrier()

    for l in range(depth):
        layer_prep(l)
        src = x_in if l == 0 else res
        phase_norm(l, src, lambda tt: [])
        if any(g in groups for g in "ABC"):
            phase_attn(l, [g for g in groups if g in "ABC"])
        if "D" in groups:
            phase_rwkv(l)
        if debug == "mixed":
            break
        if do_out:
            phase_out(l, src, out if l == depth - 1 else res)

    if debug == "mixed":
        with contextlib.ExitStack() as st:
            bounce = sbx(st, "bounce", [128, S], BF16)
            for mt in range(16):
                P.dma(bounce[:], mixed[mt], [mixed_d[mt]], [bounce.d], bounce.d)
                P.dma(dbg[mt], bounce[:], [bounce.d], [], bounce.d)
            P.barrier()
    return nc, es, P


_CACHE = {}


def kernel(**inputs):
    inp = {k: np.asarray(v) for k, v in inputs.items()}
    B, S, _ = inp["x"].shape
    depth = inp["w_in"].shape[0]
    nc, es, P = build(S=S, depth=depth)
    common = host_inputs(inp, depth)
    in_maps = []
    for b in range(B):
        m = dict(common)
        m["x"] = np.ascontiguousarray(inp["x"][b], dtype=np.float32)
        in_maps.append(m)
    res = run_bass_kernel_spmd(nc, in_maps, core_ids=list(range(B)))
    return np.stack([np.asarray(r["out"]) for r in res.results], axis=0).astype(np.float32)
```

```python
import contextlib
import math
import numpy as np
import ml_dtypes
import concourse.bass as bass
import concourse.mybir as mybir
from concourse.bass_utils import run_bass_kernel_spmd

F32 = mybir.dt.float32
BF16 = mybir.dt.bfloat16
AF = mybir.ActivationFunctionType
ALU = mybir.AluOpType
AX = mybir.AxisListType

D_MODEL = 1024
N_IN = 8264
D_MIX = 2048
NEG = -30000.0
NORM_EPS = 1e-6
RWKV_LN_EPS = 64e-5


class Dep:
    __slots__ = ("w", "r", "dsem", "dcnt")

    def __init__(self):
        self.w = None
        self.r = {}
        self.dsem = None
        self.dcnt = 0


class Prog:
    SEM_LIMIT = 30000

    def __init__(self, nc, es):
        self.nc = nc
        self.es = es
        self.eng = {"pe": nc.tensor, "act": nc.scalar, "dve": nc.vector, "pool": nc.gpsimd, "sp": nc.sync}
        self.esem = {}
        self.seen = {e: {} for e in self.eng}
        self.nsem = 0
        self.pe_sems = set()
        self.ninst = {e: 0 for e in self.eng}
        self.latest = {}
        self.dma_pool = []
        self.phase_slots = []

    def new_sem(self):
        s = self.es.enter_context(self.nc.semaphore(f"sm{self.nsem}"))
        self.nsem += 1
        return s

    def _bump(self, e):
        cur = self.esem.get(e)
        if cur is None or cur[1] >= self.SEM_LIMIT:
            cur = [self.new_sem(), 0]
            self.esem[e] = cur
            if e == "pe":
                self.pe_sems.add(cur[0])
        cur[1] += 1
        return cur[0], cur[1]

    def _collect(self, e, reads, writes):
        need = {}
        for d in reads:
            t = d.w
            if t is not None and need.get(t[0], 0) < t[1]:
                need[t[0]] = t[1]
        for d in writes:
            t = d.w
            if t is not None and need.get(t[0], 0) < t[1]:
                need[t[0]] = t[1]
            for s, v in d.r.items():
                if need.get(s, 0) < v:
                    need[s] = v
        seen = self.seen[e]
        for s, v in need.items():
            if e == "pe" and s in self.pe_sems:
                continue
            if seen.get(s, 0) >= v:
                continue
            self.eng[e].wait_ge(s, v)
            seen[s] = v

    def _record(self, tok, reads, writes):
        for d in reads:
            if d.r.get(tok[0], 0) < tok[1]:
                d.r[tok[0]] = tok[1]
        for d in writes:
            d.w = tok
            d.r = {}

    def op(self, e, fn, reads=(), writes=()):
        self._collect(e, reads, writes)
        ins = fn(self.eng[e])
        sem, cnt = self._bump(e)
        ins.then_inc(sem, 1)
        self.ninst[e] += 1
        self.latest[sem] = cnt
        self._record((sem, cnt), reads, writes)

    def dma(self, out, in_, reads, writes, slot, q="sp", persistent=False):
        self._collect(q, reads, writes)
        if slot.dsem is None or slot.dsem[1] >= 3500:
            if self.dma_pool:
                slot.dsem = self.dma_pool.pop()
            else:
                slot.dsem = [self.new_sem(), 0]
            if not persistent:
                self.phase_slots.append(slot)
        ent = slot.dsem
        ins = self.eng[q].dma_start(out=out, in_=in_)
        ent[1] += 1
        ins.then_inc(ent[0], 16)
        self.ninst[q] += 1
        self.latest[ent[0]] = 16 * ent[1]
        self._record((ent[0], 16 * ent[1]), reads, writes)

    def barrier(self):
        for e in self.eng:
            seen = self.seen[e]
            for s, v in self.latest.items():
                if seen.get(s, 0) >= v:
                    continue
                self.eng[e].wait_ge(s, v)
                seen[s] = v
        for slot in self.phase_slots:
            if slot.dsem is not None and slot.dsem[1] < 3000:
                self.dma_pool.append(slot.dsem)
            slot.dsem = None
        self.phase_slots = []

    def wait_all(self, e, deps):
        self._collect(e, deps, deps)


class T:
    def __init__(self, t):
        self.t = t
        self.d = Dep()

    def __getitem__(self, k):
        return self.t[k]


PT_NG = 0
PT_QKG = 8
PT_OG = 12
PT_FB = 28
PT_DL = 32
PT_MU = 288
PT_W0 = 301
PT_A0 = 305
PT_KK = 309
PT_KA = 313
PT_RK = 317
PT_LNG = 321
PT_LNB = 325
NPT = 336

C_IDENT = 0
C_ONES = 128
C_BLK64 = 256
C_MINCL = 384
C_MSTRICT = 512
C_NUINCL = 640
C_SU = 768
C_IU = 832
C_SL = 896
NCONST = 960

C32_IDENT = 0
C32_ONES = 128
NC32 = 256


def make_consts():
    c = np.zeros((128, NCONST), np.float32)
    p = np.arange(128)[:, None]
    j = np.arange(128)[None, :]
    c[:, C_IDENT:C_IDENT + 128] = np.eye(128)
    c[:, C_ONES:C_ONES + 128] = 1.0
    c[:, C_BLK64:C_BLK64 + 128] = ((p // 64) == (j // 64))
    c[:, C_MINCL:C_MINCL + 128] = (p <= j)
    c[:, C_MSTRICT:C_MSTRICT + 128] = (p < j)
    c[:, C_NUINCL:C_NUINCL + 128] = -1.0 * (p >= j)
    c[:, C_SU:C_SU + 64] = (p < j[:, :64])
    c[:, C_IU:C_IU + 64] = (p <= j[:, :64])
    c[:, C_SL:C_SL + 64] = (p > j[:, :64])
    return c.astype(ml_dtypes.bfloat16)


def make_consts32():
    c = np.zeros((128, NC32), np.float32)
    c[:, C32_IDENT:C32_IDENT + 128] = np.eye(128)
    c[:, C32_ONES:C32_ONES + 128] = 1.0
    return c


def t5_bucket_np(dist):
    d = np.maximum(dist, 1).astype(np.float32)
    large = 16 + (np.log(d / np.float32(16)) / np.float32(math.log(128 / 16)) * 16).astype(np.int32)
    large = np.minimum(large, 31)
    return np.where(dist < 16, dist, large)


def make_ptab(inp, depth):
    pt = np.zeros((depth, 128, NPT), np.float32)
    p = np.arange(128)
    for l in range(depth):
        pt[l, :, PT_NG:PT_NG + 8] = inp["norm_gain"][l].reshape(8, 128).T
        for i in range(4):
            pt[l, :, PT_QKG + i] = inp["qk_gain"][l, i][p % 64]
        pt[l, :, PT_OG:PT_OG + 16] = inp["out_gain"][l].reshape(16, 128).T
        pt[l, :8, PT_FB] = inp["forget_bias"][l]
        pt[l, :, PT_DL:PT_DL + 256] = inp["diff_lambda"][l].reshape(1, 256)
        mu = inp["rwkv_mu"][l]
        pt[l, :, PT_MU:PT_MU + 12] = mu[:1536].reshape(12, 128).T
        pt[l, :64, PT_MU + 12] = mu[1536:1600]
        pt[l, :, PT_W0:PT_W0 + 4] = inp["rwkv_w0"][l].reshape(4, 128).T
        pt[l, :, PT_A0:PT_A0 + 4] = inp["rwkv_a0"][l].reshape(4, 128).T
        pt[l, :, PT_KK:PT_KK + 4] = inp["rwkv_kkr"][l, 0].reshape(4, 128).T
        pt[l, :, PT_KA:PT_KA + 4] = inp["rwkv_kkr"][l, 1].reshape(4, 128).T
        pt[l, :, PT_RK:PT_RK + 4] = inp["rwkv_kkr"][l, 2].reshape(4, 128).T
        pt[l, :, PT_LNG:PT_LNG + 4] = inp["rwkv_ln_gain"][l].reshape(4, 128).T
        pt[l, :, PT_LNB:PT_LNB + 4] = inp["rwkv_ln_bias"][l].reshape(4, 128).T
    return pt


def make_relbias_tables(rel_bias):
    p = np.arange(128)[:, None]
    i = np.arange(1024)[None, :]
    dist = i - 384 - p
    bucket = t5_bucket_np(np.maximum(dist, 0))
    tbl = np.empty((4, 128, 1024), np.float32)
    for h in range(4):
        tbl[h] = np.where(dist >= 0, rel_bias[bucket, h], np.float32(NEG))
    far = np.zeros((128, 4), np.float32)
    far[:, :] = rel_bias[31][None, :]
    return tbl, far


def host_inputs(inp, depth):
    tbl, far = make_relbias_tables(np.asarray(inp["rel_bias"], np.float32))
    return {
        "w_in": np.ascontiguousarray(inp["w_in"][:depth]),
        "w_out": np.ascontiguousarray(inp["w_out"][:depth]),
        "w_up": np.ascontiguousarray(inp["rwkv_w_up"][:depth]),
        "a_up": np.ascontiguousarray(inp["rwkv_a_up"][:depth]),
        "ptab": make_ptab(inp, depth),
        "consts": make_consts(),
        "consts32": make_consts32(),
        "relb": tbl,
        "relfar": far,
    }


class Rot:
    def __init__(self, items):
        self.items = items
        self.i = 0

    def next(self):
        it = self.items[self.i % len(self.items)]
        self.i += 1
        return it


class _Stop(Exception):
    pass


def build(S=4096, depth=2, debug=None, groups="ABCD", do_out=True, dstop=0):
    nc = bass.Bass("TRN2", target_bir_lowering=False)
    es = contextlib.ExitStack()
    P = Prog(nc, es)
    NTB = S // 512
    NTT = S // 128

    def dram(name, shape, dt=F32, kind="ExternalInput"):
        return nc.dram_tensor(name, list(shape), dt, kind=kind).ap()

    uniq = [0]

    def sbx(stack, name, shape, dt=F32):
        uniq[0] += 1
        return T(stack.enter_context(nc.sbuf_tensor(f"{name}_{uniq[0]}", list(shape), dt)))

    def sb(name, shape, dt=F32):
        return sbx(es, name, shape, dt)

    x_in = dram("x", [S, D_MODEL])
    w_in = dram("w_in", [depth, D_MODEL, N_IN])
    w_out = dram("w_out", [depth, D_MIX, D_MODEL])
    w_up = dram("w_up", [depth, 32, 512])
    a_up = dram("a_up", [depth, 32, 512])
    ptab = dram("ptab", [depth, 128, NPT])
    consts = dram("consts", [128, NCONST], BF16)
    consts32 = dram("consts32", [128, NC32])
    relb = dram("relb", [4, 128, 1024])
    relfar = dram("relfar", [128, 4])
    out = dram("out", [S, D_MODEL], kind="ExternalOutput")
    mixed = dram("mixed", [16, 128, S], BF16, kind="Internal")
    mixed_d = [Dep() for _ in range(16)]
    dbg = None
    if debug == "mixed":
        dbg = dram("dbg", [16, 128, S], BF16, kind="ExternalOutput")

    cst = sb("cst", [128, NCONST], BF16)
    P.dma(cst[:], consts[:, :], [], [cst.d], cst.d, persistent=True)
    c32 = sb("c32", [128, NC32])
    P.dma(c32[:], consts32[:, :], [], [c32.d], c32.d, persistent=True)
    rfar = sb("rfar", [128, 4])
    P.dma(rfar[:], relfar[:, :], [], [rfar.d], rfar.d, persistent=True)
    ident = cst.t[:, C_IDENT:C_IDENT + 128]
    ones_b = cst.t[:, C_ONES:C_ONES + 128]
    blk64 = cst.t[:, C_BLK64:C_BLK64 + 128]
    m_incl = cst.t[:, C_MINCL:C_MINCL + 128]
    m_strict = cst.t[:, C_MSTRICT:C_MSTRICT + 128]
    nuincl = cst.t[:, C_NUINCL:C_NUINCL + 128]
    pt = [sb(f"ptab{l}", [128, NPT + 16]) for l in range(depth)]
    PT_X = NPT
    for l in range(depth):
        P.dma(pt[l].t[:, 0:NPT], ptab[l], [], [pt[l].d], pt[l].d, persistent=True)

    ps = [T(es.enter_context(nc.psum_tensor(f"ps{b}", [128, 512], F32))) for b in range(8)]
    misc_banks = Rot([ps[0], ps[1]])
    proj_banks = Rot([ps[0], ps[1], ps[4], ps[5], ps[6], ps[7]])

    hT = sb("hT", [128, 8, S], BF16)
    hT_d = [[Dep() for _ in range(NTB)] for _ in range(8)]

    dbgstate = []
    lam_init = [0.8 - 0.6 * math.exp(-0.3 * l) for l in range(depth)]

    def layer_prep(l):
        p = pt[l]
        d = p.d
        P.op("dve", lambda e: e.tensor_scalar(out=p.t[:, PT_X:PT_X + 1], in0=p.t[:, PT_QKG:PT_QKG + 1], scalar1=0.125,
                                              scalar2=None, op0=ALU.mult), [d], [d])
        P.op("dve", lambda e: e.tensor_scalar(out=p.t[:, PT_X + 1:PT_X + 2], in0=p.t[:, PT_QKG + 2:PT_QKG + 3], scalar1=0.125,
                                              scalar2=None, op0=ALU.mult), [d], [d])
        P.op("dve", lambda e: e.tensor_scalar(out=p.t[:, PT_X + 2:PT_X + 3], in0=p.t[:, PT_FB:PT_FB + 1], scalar1=-1.0,
                                              scalar2=None, op0=ALU.mult), [d], [d])
        tmp = sb(f"lamtmp{l}", [128, 132])
        P.op("dve", lambda e: e.tensor_tensor(out=tmp.t[:, 0:64], in0=p.t[:, PT_DL:PT_DL + 64], in1=p.t[:, PT_DL + 64:PT_DL + 128],
                                              op=ALU.mult), [d], [tmp.d])
        P.op("dve", lambda e: e.tensor_tensor(out=tmp.t[:, 64:128], in0=p.t[:, PT_DL + 128:PT_DL + 192],
                                              in1=p.t[:, PT_DL + 192:PT_DL + 256], op=ALU.mult), [d], [tmp.d])
        P.op("dve", lambda e: e.reduce_sum(out=tmp.t[:, 128:129], in_=tmp.t[:, 0:64], axis=AX.X), [tmp.d], [tmp.d])
        P.op("dve", lambda e: e.reduce_sum(out=tmp.t[:, 129:130], in_=tmp.t[:, 64:128], axis=AX.X), [tmp.d], [tmp.d])
        P.op("act", lambda e: e.activation(out=tmp.t[:, 130:132], in_=tmp.t[:, 128:130], func=AF.Exp), [tmp.d], [tmp.d])
        P.op("dve", lambda e: e.tensor_tensor(out=tmp.t[:, 128:129], in0=tmp.t[:, 131:132], in1=tmp.t[:, 130:131],
                                              op=ALU.subtract), [tmp.d], [tmp.d])
        P.op("dve", lambda e: e.tensor_scalar(out=p.t[:, PT_X + 3:PT_X + 4], in0=tmp.t[:, 128:129], scalar1=-lam_init[l],
                                              scalar2=None, op0=ALU.add), [tmp.d, d], [d])
        P.op("dve", lambda e: e.tensor_tensor(out=p.t[:, PT_X + 4:PT_X + 8], in0=p.t[:, PT_LNG:PT_LNG + 4],
                                              in1=p.t[:, PT_OG + 12:PT_OG + 16], op=ALU.mult), [d], [d])

    def phase_norm(l, src, src_deps):
        with contextlib.ExitStack() as st:
            xs = [sbx(st, f"xs{i}", [128, D_MODEL]) for i in range(2)]
            xn = sbx(st, "xn", [128, 4, D_MODEL], BF16)
            xn_d = [Dep() for _ in range(4)]
            junk = sbx(st, "junk", [128, D_MODEL], BF16)
            ss = sbx(st, "ss", [128, 8])
            ss_d = [Dep() for _ in range(8)]
            for tb in range(NTB):
                for j in range(4):
                    tt = tb * 4 + j
                    sl = xs[tt % 2]
                    P.dma(sl[:], src[tt * 128:(tt + 1) * 128, :], src_deps(tt), [sl.d], sl.d)
                    P.op("act", lambda e, sl=sl, j=j: e.activation(out=junk[:], in_=sl[:], func=AF.Square,
                                                                   accum_out=ss[:, j:j + 1]),
                         [sl.d], [junk.d, ss_d[j]])
                    P.op("dve", lambda e, j=j: e.tensor_scalar(out=ss[:, 4 + j:5 + j], in0=ss[:, j:j + 1],
                                                               scalar1=1.0 / D_MODEL, scalar2=NORM_EPS,
                                                               op0=ALU.mult, op1=ALU.add), [ss_d[j]], [ss_d[4 + j]])
                    P.op("act", lambda e, j=j: e.activation(out=ss[:, 4 + j:5 + j], in_=ss[:, 4 + j:5 + j], func=AF.Sqrt),
                         [ss_d[4 + j]], [ss_d[4 + j]])
                    P.op("dve", lambda e, j=j: e.reciprocal(out=ss[:, 4 + j:5 + j], in_=ss[:, 4 + j:5 + j]),
                         [ss_d[4 + j]], [ss_d[4 + j]])
                    P.op("dve", lambda e, sl=sl, j=j: e.tensor_scalar(out=xn[:, j, :], in0=sl[:], scalar1=ss[:, 4 + j:5 + j],
                                                                      scalar2=None, op0=ALU.mult),
                         [sl.d, ss_d[4 + j]], [xn_d[j]])
                for c in range(8):
                    bank = misc_banks.next()
                    pb = bank.t[:].bitcast(BF16)

                    def tr(e, c=c, pb=pb):
                        r = None
                        for j in range(4):
                            r = e.transpose(out=pb[:, j * 128:(j + 1) * 128], in_=xn[:, j, c * 128:(c + 1) * 128],
                                            identity=ident)
                        return r
                    P.op("pe", tr, xn_d + [cst.d], [bank.d])
                    P.op("act", lambda e, c=c, pb=pb, tb=tb: e.activation(out=hT[:, c, tb * 512:(tb + 1) * 512],
                                                                          in_=pb[:, 0:512], func=AF.Copy,
                                                                          scale=pt[l][:, PT_NG + c:PT_NG + c + 1]),
                         [bank.d, pt[l].d], [hT_d[c][tb]])
            P.barrier()

    wstack = contextlib.ExitStack()
    wst = [sbx(es, f"wst{i}", [128, 8, 128]) for i in range(2)]
    wbf = [sbx(es, f"wbf{i}", [128, 8, 128], BF16) for i in range(2)]
    wrot = [0]

    cast_eng = ["pool"]

    def load_w(l, col0, M):
        i = wrot[0] % 2
        wrot[0] += 1
        src = w_in[l].rearrange("(c p) n -> p c n", p=128)[:, :, col0:col0 + M]
        P.dma(wst[i].t[:, :, 0:M], src, [], [wst[i].d], wst[i].d, persistent=True)
        if cast_eng[0] == "pool":
            P.op("pool", lambda e: e.tensor_copy(out=wbf[i].t[:, :, 0:M], in_=wst[i].t[:, :, 0:M]), [wst[i].d], [wbf[i].d])
        else:
            P.op("act", lambda e: e.activation(out=wbf[i].t[:, :, 0:M], in_=wst[i].t[:, :, 0:M], func=AF.Copy), [wst[i].d], [wbf[i].d])
        return wbf[i]

    def inproj_T(l, col0, M, evac, banks=None):
        wb = load_w(l, col0, M)
        for tb in range(NTB):
            bank = (banks or proj_banks).next()

            def mm(e, bank=bank, tb=tb):
                r = None
                for c in range(8):
                    r = e.matmul(bank.t[0:M, :], lhsT=wb.t[:, c, 0:M], rhs=hT[:, c, tb * 512:(tb + 1) * 512],
                                 start=(c == 0), stop=(c == 7))
                return r
            P.op("pe", mm, [wb.d] + [hT_d[c][tb] for c in range(8)], [bank.d])
            evac(tb, bank)

    def inproj_tok(l, col0, N, evac):
        wb = load_w(l, col0, N)
        for tt in range(NTT):
            bank = proj_banks.next()

            def mm(e, bank=bank, tt=tt):
                r = None
                for c in range(8):
                    r = e.matmul(bank.t[:, 0:N], lhsT=hT[:, c, tt * 128:(tt + 1) * 128], rhs=wb.t[:, c, 0:N],
                                 start=(c == 0), stop=(c == 7))
                return r
            P.op("pe", mm, [wb.d] + [hT_d[c][tt // 4] for c in range(8)], [bank.d])
            evac(tt, bank)

    def phase_attn(l, grps):
        with contextlib.ExitStack() as st:
            QT = sbx(st, "QT", [128, S], BF16)
            KT = sbx(st, "KT", [128, S], BF16)
            GT = sbx(st, "GT", [128, S], BF16)
            VT = sbx(st, "VT", [128, NTT, 128], BF16)
            MT = sbx(st, "MT", [128, S], BF16)
            QT_d = [Dep() for _ in range(NTB)]
            KT_d = [Dep() for _ in range(NTB)]
            GT_d = [Dep() for _ in range(NTB)]
            VT_d = [Dep() for _ in range(NTT)]
            MT_d = [Dep() for _ in range(NTB)]
            f32t = Rot([sbx(st, f"f32t{i}", [128, 512]) for i in range(8)])
            b16t = Rot([sbx(st, f"b16t{i}", [128, 512], BF16) for i in range(10)])
            Rt = sbx(st, "Rt", [128, 512])
            Rt2 = sbx(st, "Rt2", [128, 512])
            tbl = sbx(st, "tbl", [128, 1024])
            biasC = sbx(st, "biasC", [128, 2, NTB, NTT])
            Gk = sbx(st, "Gk", [128, NTT, 8])
            Gmid = sbx(st, "Gmid", [128, 8 * NTB])
            zbanks = Rot([ps[2], ps[3]])

            def evac_qk_norm(dst, dst_d, gaincol):
                def ev(tb, bank):
                    sq = b16t.next()
                    P.op("act", lambda e: e.activation(out=sq[:], in_=bank[:], func=AF.Square), [bank.d], [sq.d])
                    b2 = zbanks.next()
                    P.op("pe", lambda e: e.matmul(b2[:], lhsT=blk64, rhs=sq[:], start=True, stop=True), [sq.d, cst.d], [b2.d])
                    rs = f32t.next()
                    P.op("act", lambda e: e.activation(out=rs[:], in_=b2[:], func=AF.Ln, scale=1.0 / 64, bias=NORM_EPS),
                         [b2.d], [rs.d])
                    P.op("act", lambda e: e.activation(out=rs[:], in_=rs[:], func=AF.Exp, scale=-0.5), [rs.d], [rs.d])
                    P.op("dve", lambda e: e.scalar_tensor_tensor(out=dst[:, tb * 512:(tb + 1) * 512], in0=bank[:], scalar=gaincol,
                                                                 in1=rs[:], op0=ALU.mult, op1=ALU.mult),
                         [bank.d, rs.d, pt[l].d], [dst_d[tb]])
                return ev

            def evac_scale(dst, dst_d, scale):
                def ev(tb, bank):
                    P.op("act", lambda e: e.activation(out=dst[:, tb * 512:(tb + 1) * 512], in_=bank[:], func=AF.Copy, scale=scale),
                         [bank.d], [dst_d[tb]])
                return ev

            def evac_copy_dve(dst, dst_d):
                def ev(tb, bank):
                    P.op("dve", lambda e: e.tensor_copy(out=dst[:, tb * 512:(tb + 1) * 512], in_=bank[:]), [bank.d], [dst_d[tb]])
                return ev

            def evac_silu(tb, bank):
                P.op("act", lambda e: e.activation(out=GT[:, tb * 512:(tb + 1) * 512], in_=bank[:], func=AF.Silu),
                     [bank.d], [GT_d[tb]])

            def evac_v(tt, bank):
                eng = "dve" if tt % 2 else "act"
                if eng == "dve":
                    P.op("dve", lambda e: e.tensor_copy(out=VT[:, tt, :], in_=bank[:, 0:128]), [bank.d], [VT_d[tt]])
                else:
                    P.op("act", lambda e: e.activation(out=VT[:, tt, :], in_=bank[:, 0:128], func=AF.Copy), [bank.d], [VT_d[tt]])

            eps_t = sbx(st, "eps_t", [128, 4])
            P.op("dve", lambda e: e.memset(eps_t[:, 0:1], NORM_EPS), [], [eps_t.d])
            P.op("dve", lambda e: e.memset(eps_t[:, 2:3], 1.0), [eps_t.d], [eps_t.d])

            def fox_prep():
                cfw = sbx(st, "cfw", [8, 512])
                cfl = sbx(st, "cfl", [8, 512])
                Gc = sbx(st, "Gc", [8, 512])
                ones8 = sbx(st, "ones8", [8, 512])
                carry = sbx(st, "carry", [8, NTB + 1])
                gdiag = sbx(st, "gdiag", [8, 8, NTB])
                P.op("dve", lambda e: e.memset(ones8[:], 1.0), [], [ones8.d])
                P.op("dve", lambda e: e.memset(carry[:], 0.0), [], [carry.d])
                gkbank = ps[4]

                def ev(tb, bank):
                    P.op("act", lambda e: e.activation(out=cfw[:], in_=bank[0:8, :], func=AF.Exp, scale=-1.0,
                                                       bias=pt[l].t[0:8, PT_X + 2:PT_X + 3]), [bank.d, pt[l].d], [cfw.d])
                    P.op("act", lambda e: e.activation(out=cfl[:], in_=cfw[:], func=AF.Ln, scale=1.0, bias=ones8[:, 0:1]),
                         [cfw.d, ones8.d], [cfl.d])
                    P.op("dve", lambda e: e.tensor_tensor_scan(out=Gc[:], data0=ones8[:], data1=cfl[:],
                                                               initial=carry[:, tb:tb + 1], op0=ALU.mult, op1=ALU.add),
                         [ones8.d, cfl.d, carry.d], [Gc.d])
                    P.op("dve", lambda e: e.tensor_copy(out=carry[:, tb + 1:tb + 2], in_=Gc[:, 511:512]), [Gc.d], [carry.d])
                    P.op("dve", lambda e: e.tensor_scalar(out=gdiag[:, :, tb], in0=c32.t[0:8, C32_IDENT:C32_IDENT + 8],
                                                          scalar1=Gc[:, 256:257], scalar2=None, op0=ALU.mult),
                         [Gc.d, c32.d], [gdiag.d])

                    def tr(e):
                        r = None
                        for j in range(4):
                            tt = tb * 4 + j
                            r = e.transpose(out=gkbank.t[:, tt * 8:(tt + 1) * 8], in_=Gc[:, j * 128:(j + 1) * 128],
                                            identity=c32.t[0:8, C32_IDENT:C32_IDENT + 8])
                        return r
                    P.op("pe", tr, [Gc.d, c32.d], [gkbank.d])
                inproj_T(l, 6144, 8, ev, banks=misc_banks)
                P.op("dve", lambda e: e.tensor_copy(out=Gk[:].rearrange("p a b -> p (a b)"), in_=gkbank.t[:, 0:NTT * 8]),
                     [gkbank.d], [Gk.d])
                b2 = ps[5]
                P.op("pe", lambda e: e.matmul(b2.t[:, 0:8 * NTB], lhsT=c32.t[0:8, C32_ONES:C32_ONES + 128],
                                              rhs=gdiag[:].rearrange("p a b -> p (a b)"), start=True, stop=True),
                     [gdiag.d, c32.d], [b2.d])
                P.op("dve", lambda e: e.tensor_copy(out=Gmid[:], in_=b2.t[:, 0:8 * NTB]), [b2.d], [Gmid.d])

            def rms_gate_store(osb, qt, M_ones, inv_n, c2, gaincol):
                if debug == "osb" and not dbgstate:
                    dbgstate.append(1)
                    dd = dram("dbg_osb", [128, 512], F32, kind="ExternalOutput")
                    P.dma(dd, osb[:], [osb.d], [], osb.d)
                    for i_, t_ in enumerate(b16t.items):
                        dd = dram(f"dbg_b16_{i_}", [128, 512], BF16, kind="ExternalOutput")
                        P.dma(dd, t_[:], [t_.d], [], t_.d)
                sq = b16t.next()
                P.op("act", lambda e: e.activation(out=sq[:], in_=osb[:], func=AF.Square), [osb.d], [sq.d])
                b2 = misc_banks.next()
                P.op("pe", lambda e: e.matmul(b2[:], lhsT=M_ones, rhs=sq[:], start=True, stop=True), [sq.d, cst.d], [b2.d])
                rs = f32t.next()
                P.op("act", lambda e: e.activation(out=rs[:], in_=b2[:], func=AF.Ln, scale=inv_n / c2,
                                                   bias=NORM_EPS / c2), [b2.d], [rs.d])
                P.op("act", lambda e: e.activation(out=rs[:], in_=rs[:], func=AF.Exp, scale=-0.5), [rs.d], [rs.d])
                P.op("dve", lambda e: e.tensor_tensor(out=rs[:], in0=rs[:], in1=osb[:], op=ALU.mult), [rs.d, osb.d], [rs.d])
                P.op("dve", lambda e: e.scalar_tensor_tensor(out=MT[:, qt * 512:(qt + 1) * 512], in0=rs[:], scalar=gaincol,
                                                             in1=GT[:, qt * 512:(qt + 1) * 512], op0=ALU.mult, op1=ALU.mult),
                     [rs.d, GT_d[qt], pt[l].d], [MT_d[qt]])

            def ktiles(qt):
                res = []
                for kt in range(4 * qt + 4):
                    c0 = 128 * (kt - 4 * qt) if kt >= 4 * qt else 0
                    res.append((kt, c0))
                return res

            for grp in grps:
                base = {"A": 0, "B": 2048, "C": 4096}[grp]
                if grp == "C":
                    fox_prep()
                c2 = 1.0
                if grp == "A":
                    c2 = (1.0 - lam_init[l]) ** 2
                P.op("dve", lambda e, c2=c2: e.memset(eps_t[:, 1:2], NORM_EPS / c2), [eps_t.d], [eps_t.d])
                for u in range(4):
                    mt = {"A": 0, "B": 4, "C": 8}[grp] + u
                    qcol, kcol, vcol, gcol = base + u * 128, base + 512 + u * 128, base + 1024 + u * 128, base + 1536 + u * 128
                    if grp == "A":
                        inproj_T(l, qcol, 128, evac_qk_norm(QT, QT_d, pt[l].t[:, PT_X:PT_X + 1]))
                        inproj_T(l, kcol, 128, evac_qk_norm(KT, KT_d, pt[l].t[:, PT_QKG + 1:PT_QKG + 2]))
                        P.dma(tbl[:], relb[u], [], [tbl.d], tbl.d)
                    elif grp == "C":
                        inproj_T(l, qcol, 128, evac_qk_norm(QT, QT_d, pt[l].t[:, PT_X + 1:PT_X + 2]))
                        inproj_T(l, kcol, 128, evac_qk_norm(KT, KT_d, pt[l].t[:, PT_QKG + 3:PT_QKG + 4]))
                        for hh in range(2):
                            h = 2 * u + hh
                            for qt in range(NTB):
                                P.op("dve", lambda e, hh=hh, h=h, qt=qt: e.tensor_scalar(
                                    out=biasC[:, hh, qt, :], in0=Gk[:, :, h], scalar1=Gmid[:, h * NTB + qt:h * NTB + qt + 1],
                                    scalar2=None, op0=ALU.subtract), [Gk.d, Gmid.d], [biasC.d])
                    else:
                        inproj_T(l, qcol, 128, evac_scale(QT, QT_d, 0.125))
                        inproj_T(l, kcol, 128, evac_copy_dve(KT, KT_d))
                    inproj_T(l, gcol, 128, evac_silu)
                    inproj_tok(l, vcol, 128, evac_v)
                    if debug == "unit":
                        P.barrier()
                        for nm, tt_, shp, dt_ in [("QT", QT, [128, S], BF16), ("KT", KT, [128, S], BF16), ("GT", GT, [128, S], BF16),
                                                 ("VT", VT, [128, NTT, 128], BF16), ("biasC", biasC, [128, 2, NTB, NTT], F32),
                                                 ("Gk", Gk, [128, NTT, 8], F32), ("Gmid", Gmid, [128, 8 * NTB], F32)]:
                            dd = dram("dbg_" + nm, shp, dt_, kind="ExternalOutput")
                            P.dma(dd, tt_.t[:], [], [], tt_.d)
                        P.barrier()
                        return

                    Rts = [Rt, Rt2]
                    for qt in range(NTB):
                        qs = qt * 512
                        kts = ktiles(qt)
                        if grp in ("A", "C"):
                            zr = [Rot([ps[2], ps[3]]), Rot([ps[0], ps[1]])]
                            if grp == "A":
                                subs = [dict(b0=64 * c, vsl=slice(0, 128), orow=slice(0, 128), ot=ps[4 + 2 * c], den=ps[5 + 2 * c], hh=c)
                                        for c in range(2)]
                            else:
                                pair = (ps[4], ps[5]) if qt % 2 == 0 else (ps[6], ps[7])
                                subs = [dict(b0=64 * hh, vsl=slice(64 * hh, 64 * hh + 64), orow=slice(64 * hh, 64 * hh + 64),
                                             ot=pair[0], den=pair[1], hh=hh) for hh in range(2)]
                            n = len(kts)
                            pend = {}
                            LAG = 2
                            for idx in range(n + LAG):
                                if idx < n:
                                    kt, c0 = kts[idx]
                                    zbs = []
                                    for si, sub in enumerate(subs):
                                        b0 = sub["b0"]
                                        zb = zr[si].next()
                                        zbs.append(zb)
                                        P.op("pe", lambda e, zb=zb, kt=kt, c0=c0, b0=b0: e.matmul(
                                            zb.t[:, c0:512], lhsT=KT[b0:b0 + 64, kt * 128:(kt + 1) * 128],
                                            rhs=QT[b0:b0 + 64, qs + c0:qs + 512], start=True, stop=True),
                                            [KT_d[kt // 4], QT_d[qt]], [zb.d])
                                    for si, sub in enumerate(subs):
                                        zb = zbs[si]
                                        ptile = b16t.next()
                                        if grp == "A":
                                            near = kt >= 4 * qt - 1
                                            if near:
                                                off = qs - kt * 128 + 384
                                                tmp = f32t.next()
                                                P.op("dve", lambda e, tmp=tmp, zb=zb, c0=c0, off=off: e.tensor_tensor(
                                                    out=tmp.t[:, c0:512], in0=zb.t[:, c0:512], in1=tbl.t[:, off + c0:off + 512],
                                                    op=ALU.add), [zb.d, tbl.d], [tmp.d])
                                                P.op("act", lambda e, tmp=tmp, ptile=ptile, c0=c0: e.activation(
                                                    out=ptile.t[:, c0:512], in_=tmp.t[:, c0:512], func=AF.Exp), [tmp.d], [ptile.d])
                                            else:
                                                P.op("act", lambda e, zb=zb, ptile=ptile, c0=c0: e.activation(
                                                    out=ptile.t[:, c0:512], in_=zb.t[:, c0:512], func=AF.Exp,
                                                    bias=rfar.t[:, u:u + 1]), [zb.d, rfar.d], [ptile.d])
                                        else:
                                            hh = sub["hh"]
                                            P.op("act", lambda e, zb=zb, ptile=ptile, c0=c0, hh=hh, kt=kt: e.activation(
                                                out=ptile.t[:, c0:512], in_=zb.t[:, c0:512], func=AF.Exp,
                                                bias=biasC[:, hh, qt, kt:kt + 1]), [zb.d, biasC.d], [ptile.d])
                                            if kt >= 4 * qt:
                                                P.op("pool", lambda e, ptile=ptile, c0=c0: e.tensor_tensor(
                                                    out=ptile.t[:, c0:c0 + 128], in0=ptile.t[:, c0:c0 + 128], in1=m_incl,
                                                    op=ALU.mult), [ptile.d, cst.d], [ptile.d])
                                        pend[(idx, si)] = (kt, c0, ptile)
                                if idx >= LAG:
                                    for si, sub in enumerate(subs):
                                        kt, c0, ptile = pend.pop((idx - LAG, si))

                                        def pv(e, kt=kt, c0=c0, ptile=ptile, sub=sub):
                                            e.matmul(sub["ot"].t[sub["orow"], c0:512], lhsT=VT[:, kt, sub["vsl"]],
                                                     rhs=ptile.t[:, c0:512], start=(kt == 0), stop=(kt == 4 * qt + 3),
                                                     skip_group_check=True)
                                            M = sub["orow"].stop - sub["orow"].start
                                            return e.matmul(sub["den"].t[sub["orow"], c0:512], lhsT=ones_b[:, 0:M],
                                                            rhs=ptile.t[:, c0:512], start=(kt == 0), stop=(kt == 4 * qt + 3),
                                                            skip_group_check=True)
                                        P.op("pe", pv, [VT_d[kt], ptile.d, cst.d], [sub["ot"].d, sub["den"].d])
                            fin_bank = Rot([ps[1], ps[0]])
                            if grp == "A":
                                r0 = f32t.next()
                                r1 = f32t.next()
                                o0 = f32t.next()
                                o1 = f32t.next()
                                P.op("act", lambda e, r0=r0: e.activation(out=r0[:], in_=ps[5][:], func=AF.Ln), [ps[5].d], [r0.d])
                                P.op("dve", lambda e, o0=o0: e.tensor_copy(out=o0[:], in_=ps[4][:]), [ps[4].d], [o0.d])
                                P.op("act", lambda e, r1=r1: e.activation(out=r1[:], in_=ps[7][:], func=AF.Ln), [ps[7].d], [r1.d])
                                P.op("dve", lambda e, o1=o1: e.tensor_copy(out=o1[:], in_=ps[6][:]), [ps[6].d], [o1.d])
                                P.op("act", lambda e, r0=r0: e.activation(out=r0[:], in_=r0[:], func=AF.Exp, scale=-1.0), [r0.d], [r0.d])
                                P.op("act", lambda e, r1=r1: e.activation(out=r1[:], in_=r1[:], func=AF.Exp, scale=-1.0), [r1.d], [r1.d])
                                P.op("pool", lambda e, r0=r0, o0=o0: e.tensor_tensor(out=r0[:], in0=o0[:], in1=r0[:], op=ALU.mult),
                                     [o0.d, r0.d], [r0.d])
                                P.op("pool", lambda e, r1=r1, o1=o1: e.tensor_tensor(out=r1[:], in0=o1[:], in1=r1[:], op=ALU.mult),
                                     [o1.d, r1.d], [r1.d])
                                osb = f32t.next()
                                P.op("dve", lambda e, r0=r0, r1=r1, osb=osb: e.scalar_tensor_tensor(
                                    out=osb[:], in0=r1[:], scalar=pt[l].t[:, PT_X + 3:PT_X + 4], in1=r0[:], op0=ALU.mult, op1=ALU.add),
                                    [r0.d, r1.d, pt[l].d], [osb.d])
                                rms_gate_store(osb, qt, ones_b, 1.0 / 128, c2, pt[l].t[:, PT_OG + mt:PT_OG + mt + 1])
                            else:
                                ot, den = subs[0]["ot"], subs[0]["den"]
                                r0 = f32t.next()
                                P.op("dve", lambda e, r0=r0, den=den: e.reciprocal(out=r0[:], in_=den[:]), [den.d], [r0.d])
                                osb = f32t.next()
                                P.op("dve", lambda e, r0=r0, ot=ot, osb=osb: e.tensor_tensor(out=osb[:], in0=ot[:], in1=r0[:], op=ALU.mult),
                                     [ot.d, r0.d], [osb.d])
                                rms_gate_store(osb, qt, blk64, 1.0 / 64, 1.0, pt[l].t[:, PT_OG + mt:PT_OG + mt + 1])
                        else:
                            ot = ps[7]
                            zr = [Rot([ps[2], ps[3]]), Rot([ps[4], ps[5]])]
                            csbs = [ps[6], ps[0]]
                            kts_r = kts[::-1]
                            n = len(kts_r)
                            for hh in range(2):
                                P.op("pool", lambda e, hh=hh: e.memset(Rts[hh][:], 0.0), [], [Rts[hh].d])
                            st1 = {}
                            st2 = {}
                            for idx in range(n + 2):
                                if idx < n:
                                    kt, c0 = kts_r[idx]
                                    zbs = []
                                    for hh in range(2):
                                        b0 = 64 * hh
                                        zb = zr[hh].next()
                                        zbs.append(zb)
                                        P.op("pe", lambda e, zb=zb, kt=kt, c0=c0, b0=b0: e.matmul(
                                            zb.t[:, c0:512], lhsT=KT[b0:b0 + 64, kt * 128:(kt + 1) * 128],
                                            rhs=QT[b0:b0 + 64, qs + c0:qs + 512], start=True, stop=True, skip_group_check=True),
                                            [KT_d[kt // 4], QT_d[qt]], [zb.d])
                                    for hh in range(2):
                                        zb = zbs[hh]
                                        ee = f32t.next()
                                        P.op("act", lambda e, zb=zb, ee=ee, c0=c0: e.activation(
                                            out=ee.t[:, c0:512], in_=zb.t[:, c0:512], func=AF.Exp), [zb.d], [ee.d])
                                        sp = b16t.next()
                                        P.op("act", lambda e, sp=sp, ee=ee, c0=c0: e.activation(
                                            out=sp.t[:, c0:512], in_=ee.t[:, c0:512], func=AF.Ln, scale=1.0, bias=1.0),
                                            [ee.d], [sp.d])
                                        if kt >= 4 * qt:
                                            P.op("pool", lambda e, sp=sp, c0=c0: e.tensor_tensor(
                                                out=sp.t[:, c0:c0 + 128], in0=sp.t[:, c0:c0 + 128], in1=m_strict, op=ALU.mult),
                                                [sp.d, cst.d], [sp.d])
                                        st1[(idx, hh)] = (kt, c0, sp, zb)
                                if 1 <= idx <= n:
                                    for hh in range(2):
                                        kt, c0, sp, db = st1.pop((idx - 1, hh))
                                        csb = csbs[hh]
                                        Rh = Rts[hh]

                                        def dmm(e, db=db, kt=kt, c0=c0, sp=sp, csb=csb):
                                            e.matmul(db.t[:, c0:512], lhsT=nuincl, rhs=sp.t[:, c0:512], start=False, stop=True,
                                                     skip_group_check=True)
                                            return e.matmul(csb.t[:, c0:512], lhsT=ones_b, rhs=sp.t[:, c0:512], start=True, stop=True)
                                        P.op("pe", dmm, [sp.d, cst.d], [db.d, csb.d])
                                        tmp = f32t.next()
                                        P.op("dve", lambda e, tmp=tmp, db=db, c0=c0, Rh=Rh: e.tensor_tensor(
                                            out=tmp.t[:, c0:512], in0=db.t[:, c0:512], in1=Rh.t[:, c0:512], op=ALU.subtract),
                                            [db.d, Rh.d], [tmp.d])
                                        P.op("dve", lambda e, c0=c0, Rh=Rh, csb=csb: e.tensor_tensor(
                                            out=Rh.t[:, c0:512], in0=csb.t[:, c0:512], in1=Rh.t[:, c0:512], op=ALU.add),
                                            [csb.d, Rh.d], [Rh.d])
                                        at = b16t.next()
                                        P.op("act", lambda e, at=at, tmp=tmp, c0=c0: e.activation(
                                            out=at.t[:, c0:512], in_=tmp.t[:, c0:512], func=AF.Exp), [tmp.d], [at.d])
                                        if kt >= 4 * qt:
                                            P.op("pool", lambda e, at=at, c0=c0: e.tensor_tensor(
                                                out=at.t[:, c0:c0 + 128], in0=at.t[:, c0:c0 + 128], in1=m_strict, op=ALU.mult),
                                                [at.d, cst.d], [at.d])
                                        st2[(idx, hh)] = (kt, c0, at)
                                if idx >= 2:
                                    for hh in range(2):
                                        orow = slice(64 * hh, 64 * hh + 64)
                                        kt, c0, at = st2.pop((idx - 1, hh))
                                        P.op("pe", lambda e, kt=kt, c0=c0, at=at, orow=orow, first=(idx == 2), last=(idx == n + 1): e.matmul(
                                            ot.t[orow, c0:512], lhsT=VT[:, kt, orow], rhs=at.t[:, c0:512],
                                            start=first, stop=last, skip_group_check=True), [VT_d[kt], at.d], [ot.d])
                            osb = f32t.next()
                            P.op("act", lambda e, osb=osb: e.activation(out=osb[:], in_=ot[:], func=AF.Copy), [ot.d], [osb.d])
                            rms_gate_store(osb, qt, blk64, 1.0 / 64, 1.0, pt[l].t[:, PT_OG + mt:PT_OG + mt + 1])
                    P.dma(mixed[mt], MT[:], MT_d, [mixed_d[mt]], MT.d)
            P.barrier()

    def phase_rwkv(l):
        cast_eng[0] = "act"
        p = pt[l]
        su = cst.t[0:64, C_SU:C_SU + 64]
        iu = cst.t[0:64, C_IU:C_IU + 64]
        slm = cst.t[0:64, C_SL:C_SL + 64]
        id64 = cst.t[0:64, C_IDENT:C_IDENT + 64]
        with contextlib.ExitStack() as st:
            zb_ = Rot([ps[2], ps[3]])
            LW = sbx(st, "LW", [64, 512], BF16)
            rmask = sbx(st, "rmask", [128, 512])
            P.op("pool", lambda e: e.memset(rmask[:], 1.0), [], [rmask.d])
            P.op("pool", lambda e: e.memset(rmask.t[:].rearrange("p (c j) -> p c j", j=64)[:, :, 0:1], 0.0), [rmask.d], [rmask.d])
            carry = sbx(st, "carry", [128, 16])
            P.op("pool", lambda e: e.memset(carry[:], 0.0), [], [carry.d])
            MS = sbx(st, "MS", [64, 8, 64])
            MSbs = [sbx(st, f"MSb{i}", [64, 8, 64], BF16) for i in range(2)]
            P.op("pool", lambda e: e.memset(MS[:], 0.0), [], [MS.d])
            for MSb_ in MSbs:
                P.op("pool", lambda e, MSb_=MSb_: e.memset(MSb_[:], 0.0), [], [MSb_.d])
            epsd = sbx(st, "epsd", [128, 2])
            P.op("pool", lambda e: e.memset(epsd[:, 0:1], RWKV_LN_EPS), [], [epsd.d])
            ARt = sbx(st, "ARt", [128, 8, 128], BF16)
            AR = [ARt for c in range(4)]
            ARl = [sbx(st, f"ARl{c}", [64, 2, 8, 128], BF16) for c in range(4)]
            BKl = sbx(st, "BKl", [64, 2, 8, 128], BF16)
            PCl = sbx(st, "PCl", [64, 8, 8])
            Vt = [sbx(st, f"Vt{c}", [64, 8, 128], BF16) for c in range(4)]
            Bh = [sbx(st, f"Bh{c}", [64, 8, 128], BF16) for c in range(4)]
            Kh = [sbx(st, f"Kh{c}", [64, 8, 128], BF16) for c in range(4)]
            ArbT = [sbx(st, f"ArbT{c}", [64, 16, 64], BF16) for c in range(4)]
            AakT = [sbx(st, f"AakT{c}", [64, 16, 64], BF16) for c in range(4)]
            ArkT = [sbx(st, f"ArkT{c}", [64, 16, 64], BF16) for c in range(4)]
            TTm = [sbx(st, f"TTm{c}", [64, 16, 64], BF16) for c in range(4)]
            PCv = sbx(st, "PCv", [128, 4, 8])
            gtd = [sbx(st, f"gtd{c}", [128, 512], BF16) for c in range(4)]
            bon = [sbx(st, f"bon{c}", [128, 512], BF16) for c in range(4)]
            BK = sbx(st, "BK", [128, 8, 128], BF16)
            uraw = sbx(st, "uraw", [128, 513])
            rp = sbx(st, "rp", [128, 512])
            kp = sbx(st, "kp", [128, 512])
            vp = sbx(st, "vp", [128, 512])
            lb = sbx(st, "lb", [64, 512], BF16)
            tt_ = [sbx(st, f"dt{i}", [128, 512]) for i in range(7)]

            class _View:
                def __init__(self, base):
                    self.t = base.t[0:64, :].bitcast(BF16).rearrange("p (a b) -> p a b", b=64)
                    self.d = base.d

                def __getitem__(self, k):
                    return self.t[k]
            Yp = [_View(tt_[0]), _View(tt_[1])]
            Zp = [_View(tt_[2]), _View(tt_[3])]
            P.dma(tt_[4].t[0:32, :], w_up[l], [], [tt_[4].d], tt_[4].d)
            P.dma(tt_[4].t[32:64, :], a_up[l], [], [tt_[4].d], tt_[4].d)
            P.op("pool", lambda e: e.tensor_copy(out=LW[:], in_=tt_[4].t[0:64, :]), [tt_[4].d], [LW.d])
            bt_ = Rot([sbx(st, f"db{i}", [128, 512], BF16) for i in range(3)])
            R1 = sbx(st, "R1", [64, 512], BF16)
            Ub = sbx(st, "Ub", [64, 512], BF16)
            MTd = sbx(st, "MTd", [128, 512], BF16)
            ysb = tt_[2]

            def TTo(eng, out, in0, in1, op, rd, wr):
                P.op(eng, lambda e: e.tensor_tensor(out=out, in0=in0, in1=in1, op=op), rd, wr)

            def TSo(eng, out, in0, s1, s2, op0, op1, rd, wr):
                P.op(eng, lambda e: e.tensor_scalar(out=out, in0=in0, scalar1=s1, scalar2=s2, op0=op0, op1=op1), rd, wr)

            def STTo(out, in0, sc, in1, op0, op1, rd, wr):
                P.op("dve", lambda e: e.scalar_tensor_tensor(out=out, in0=in0, scalar=sc, in1=in1, op0=op0, op1=op1), rd, wr)

            def ACTo(out, in_, func, rd, wr, scale=1.0, bias=None):
                if bias is None:
                    P.op("act", lambda e: e.activation(out=out, in_=in_, func=func, scale=scale), rd, wr)
                else:
                    P.op("act", lambda e: e.activation(out=out, in_=in_, func=func, scale=scale, bias=bias), rd, wr)

            def seg_inproj(col0, M, tb, evac):
                wb = load_w(l, col0, M)
                bank = misc_banks.next()

                def mm(e):
                    r = None
                    for c in range(8):
                        r = e.matmul(bank.t[0:M, :], lhsT=wb.t[:, c, 0:M], rhs=hT[:, c, tb * 512:(tb + 1) * 512],
                                     start=(c == 0), stop=(c == 7))
                    return r
                P.op("pe", mm, [wb.d] + [hT_d[c][tb] for c in range(8)], [bank.d])
                evac(bank)

            def shifted(bank, M, cidx, mucol, dst, dst_dep):
                P.op("act", lambda e: e.activation(out=uraw.t[0:M, 1:513], in_=bank.t[0:M, :], func=AF.Copy), [bank.d], [uraw.d])
                P.op("pool", lambda e: e.tensor_copy(out=uraw.t[0:M, 0:1], in_=carry.t[0:M, cidx:cidx + 1]), [carry.d, uraw.d], [uraw.d])
                P.op("pool", lambda e: e.tensor_copy(out=carry.t[0:M, cidx:cidx + 1], in_=uraw.t[0:M, 512:513]), [uraw.d], [carry.d])
                d_ = tt_[6]
                TTo("dve", d_.t[0:M, :], uraw.t[0:M, 0:512], uraw.t[0:M, 1:513], ALU.subtract, [uraw.d], [d_.d])
                STTo(dst, d_.t[0:M, :], mucol, uraw.t[0:M, 1:513], ALU.mult, ALU.add, [d_.d, uraw.d, p.d], [dst_dep])

            v3 = lambda ap: ap.rearrange("p (c j) -> p c j", j=64)

            def chk(k):
                if dstop == k:
                    raise _Stop()

            def seg_body(tb):
                def ev_lora(bank):
                    l32 = tt_[5]
                    shifted(bank, 64, 12, p.t[0:64, PT_MU + 12:PT_MU + 13], l32.t[0:64, :], l32.d)
                    ACTo(lb.t[0:32, :], l32.t[0:32, :], AF.Tanh, [l32.d], [lb.d])
                    P.op("pool", lambda e: e.tensor_copy(out=lb.t[32:64, :], in_=l32.t[32:64, :]), [l32.d, lb.d], [lb.d])
                seg_inproj(6152 + 1536, 64, tb, ev_lora)
                chk(1)
                def d1(ct):
                    seg_inproj(6152 + ct * 128, 128, tb,
                               lambda bank: shifted(bank, 128, ct, p.t[:, PT_MU + ct:PT_MU + ct + 1], rp[:], rp.d))
                    seg_inproj(6152 + 512 + ct * 128, 128, tb,
                               lambda bank: shifted(bank, 128, 4 + ct, p.t[:, PT_MU + 4 + ct:PT_MU + 5 + ct], kp[:], kp.d))
                    seg_inproj(6152 + 1024 + ct * 128, 128, tb,
                               lambda bank: shifted(bank, 128, 8 + ct, p.t[:, PT_MU + 8 + ct:PT_MU + 9 + ct], vp[:], vp.d))
                    seg_inproj(7752 + ct * 128, 128, tb,
                               lambda bank: ACTo(gtd[ct][:], bank[:], AF.Silu, [bank.d], [gtd[ct].d]))

                def d2(ct):
                    chk(2)
                    t1, t2, t3, t4, t5, t6, t7 = tt_
                    b1 = zb_.next()
                    P.op("pe", lambda e: e.matmul(b1[:], lhsT=LW.t[0:32, ct * 128:(ct + 1) * 128], rhs=lb.t[0:32, :], start=True, stop=True),
                         [LW.d, lb.d], [b1.d])
                    ACTo(t1[:], b1[:], AF.Sigmoid, [b1.d, p.d], [t1.d], bias=p.t[:, PT_W0 + ct:PT_W0 + ct + 1])
                    b2 = zb_.next()
                    P.op("pe", lambda e: e.matmul(b2[:], lhsT=LW.t[32:64, ct * 128:(ct + 1) * 128], rhs=lb.t[32:64, :], start=True, stop=True),
                         [LW.d, lb.d], [b2.d])
                    ACTo(t2[:], b2[:], AF.Sigmoid, [b2.d, p.d], [t2.d], bias=p.t[:, PT_A0 + ct:PT_A0 + ct + 1])
                    P.op("dve", lambda e: e.tensor_tensor_scan(out=t3[:], data0=rmask[:], data1=t1[:], initial=0.0,
                                                               op0=ALU.mult, op1=ALU.add), [rmask.d, t1.d], [t3.d])
                    ACTo(PCv.t[:, ct, :], v3(t3[:])[:, :, 63], AF.Exp, [t3.d], [PCv.d], scale=-0.6065306597126334)
                    ACTo(t4[:], t3[:], AF.Exp, [t3.d], [t4.d], scale=-0.6065306597126334)
                    TTo("pool", AR[ct].t[:, :, 64:128], v3(rp[:]), v3(t4[:]), ALU.mult, [rp.d, t4.d], [AR[ct].d])
                    chk(3)
                    TSo("dve", t5[:], kp[:], p.t[:, PT_KK + ct:PT_KK + ct + 1], None, ALU.mult, ALU.bypass, [kp.d, p.d], [t5.d])
                    sq = bt_.next()
                    ACTo(sq[:], t5[:], AF.Square, [t5.d], [sq.d])
                    b3 = zb_.next()
                    P.op("pe", lambda e: e.matmul(b3[:], lhsT=blk64, rhs=sq[:], start=True, stop=True), [sq.d, cst.d], [b3.d])
                    ACTo(t6[:], b3[:], AF.Sqrt, [b3.d], [t6.d])
                    TSo("dve", t6[:], t6[:], 1e-12, None, ALU.max, ALU.bypass, [t6.d], [t6.d])
                    ACTo(t6[:], t6[:], AF.Ln, [t6.d], [t6.d])
                    ACTo(t6[:], t6[:], AF.Exp, [t6.d], [t6.d], scale=-1.0)
                    TTo("pool", t5[:], t5[:], t6[:], ALU.mult, [t5.d, t6.d], [t5.d])
                    TTo("dve", t7[:], t3[:], t1[:], ALU.subtract, [t3.d, t1.d], [t7.d])
                    ACTo(t7[:], t7[:], AF.Exp, [t7.d], [t7.d], scale=-0.6065306597126334)
                    STTo(AR[ct].t[:, :, 0:64], v3(t5[:]), -1.0, v3(t7[:]), ALU.mult, ALU.mult, [t5.d, t7.d], [AR[ct].d])
                    TSo("dve", t6[:], t2[:], 1.0, p.t[:, PT_KA + ct:PT_KA + ct + 1], ALU.subtract, ALU.mult, [t2.d, p.d], [t6.d])
                    STTo(t6[:], t6[:], 1.0, kp[:], ALU.add, ALU.mult, [t6.d, kp.d], [t6.d])
                    TTo("pool", t5[:], t5[:], t2[:], ALU.mult, [t5.d, t2.d], [t5.d])
                    ACTo(t4[:], t3[:], AF.Exp, [t3.d], [t4.d], scale=0.6065306597126334)
                    TTo("pool", BK.t[:, :, 64:128], v3(t6[:]), v3(t4[:]), ALU.mult, [t6.d, t4.d], [BK.d])
                    TTo("dve", BK.t[:, :, 0:64], v3(t5[:]), v3(t4[:]), ALU.mult, [t5.d, t4.d], [BK.d])
                    chk(4)
                    cumC = v3(t3[:])[:, :, 63:64].to_broadcast([128, 8, 64])
                    TTo("dve", v3(t7[:]), cumC, v3(t3[:]), ALU.subtract, [t3.d, t7.d], [t7.d])
                    ACTo(t7[:], t7[:], AF.Exp, [t7.d], [t7.d], scale=-0.6065306597126334)
                    khc = bt_.next()
                    TTo("pool", khc[:], t6[:], t7[:], ALU.mult, [t6.d, t7.d], [khc.d])
                    bhc = bt_.next()
                    TTo("dve", bhc[:], t5[:], t7[:], ALU.mult, [t5.d, t7.d], [bhc.d])
                    vbc = bt_.next()
                    P.op("pool", lambda e: e.tensor_copy(out=vbc[:], in_=vp[:]), [vp.d], [vbc.d])
                    chk(45)
                    for src_, dst_ in ((khc, Kh[ct]), (bhc, Bh[ct]), (vbc, Vt[ct])):
                        bk_ = zb_.next()
                        pb = bk_.t[:].bitcast(BF16)

                        def trs(e, src_=src_, pb=pb):
                            r = None
                            for cc in range(8):
                                r = e.transpose(out=pb[0:64, cc * 128:(cc + 1) * 128], in_=src_.t[:, cc * 64:(cc + 1) * 64],
                                                identity=ident)
                            return r
                        P.op("pe", trs, [src_.d, cst.d], [bk_.d])
                        chk(46)
                        P.op("act", lambda e, dst_=dst_, pb=pb: e.activation(out=dst_.t[:].rearrange("p a b -> p (a b)"),
                                                                             in_=pb[0:64, 0:1024], func=AF.Copy), [bk_.d], [dst_.d])
                        chk(47)
                        if dst_ is Bh[ct]:
                            chk(48)
                    chk(5)
                    prb = bt_.next()
                    STTo(prb[:], rp[:], p.t[:, PT_RK + ct:PT_RK + ct + 1], t6[:], ALU.mult, ALU.mult, [rp.d, t6.d, p.d], [prb.d])
                    b4 = zb_.next()
                    P.op("pe", lambda e: e.matmul(b4[:], lhsT=blk64, rhs=prb[:], start=True, stop=True), [prb.d, cst.d], [b4.d])
                    TTo("dve", bon[ct][:], b4[:], vp[:], ALU.mult, [b4.d, vp.d], [bon[ct].d])

                def d3(ct):
                    chk(6)
                    for hh in range(2):
                        P.dma(ARl[ct].t[:, hh], ARt.t[64 * hh:64 * hh + 64], [ARt.d], [ARl[ct].d], ARl[ct].d)
                        P.dma(BKl.t[:, hh], BK.t[64 * hh:64 * hh + 64], [BK.d], [BKl.d], BKl.d)
                    for grp4 in range(4):
                        bkO = zb_.next()

                        def mmO(e, grp4=grp4, bkO=bkO, lo=0):
                            r = None
                            for i4 in range(4):
                                blk = grp4 * 4 + i4
                                cc, hh = blk // 2, blk % 2
                                r = e.matmul(bkO.t[0:64, i4 * 128:(i4 + 1) * 128], lhsT=BKl.t[:, hh, cc, lo:lo + 64],
                                             rhs=ARl[ct].t[:, hh, cc, :], start=True, stop=True)
                            return r
                        P.op("pe", mmO, [BKl.d, ARl[ct].d], [bkO.d])
                        chk(61)
                        o4 = bkO.t[0:64, :].rearrange("p (b c) -> p b c", c=128)
                        sub = slice(grp4 * 4, grp4 * 4 + 4)
                        TTo("dve", Yp[0].t[:, sub, :], o4[:, :, 0:64], su.unsqueeze(1).to_broadcast([64, 4, 64]), ALU.mult,
                            [bkO.d, cst.d], [Yp[0].d])
                        chk(62)
                        TTo("dve", ArbT[ct].t[:, sub, :], o4[:, :, 64:128], iu.unsqueeze(1).to_broadcast([64, 4, 64]), ALU.mult,
                            [bkO.d, cst.d], [ArbT[ct].d])
                        bkK = zb_.next()
                        P.op("pe", lambda e, grp4=grp4, bkK=bkK: mmO(e, grp4, bkK, 64), [BKl.d, ARl[ct].d], [bkK.d])
                        k4 = bkK.t[0:64, :].rearrange("p (b c) -> p b c", c=128)
                        TTo("dve", AakT[ct].t[:, sub, :], k4[:, :, 0:64], su.unsqueeze(1).to_broadcast([64, 4, 64]), ALU.mult,
                            [bkK.d, cst.d], [AakT[ct].d])
                        TTo("dve", ArkT[ct].t[:, sub, :], k4[:, :, 64:128], iu.unsqueeze(1).to_broadcast([64, 4, 64]), ALU.mult,
                            [bkK.d, cst.d], [ArkT[ct].d])
                    for half in range(2):
                        bkN = zb_.next()

                        def mmN(e, half=half, bkN=bkN):
                            r = None
                            for i8 in range(8):
                                blk = half * 8 + i8
                                cc, hh = blk // 2, blk % 2
                                r = e.matmul(bkN.t[0:64, i8 * 64:(i8 + 1) * 64], lhsT=ARl[ct].t[:, hh, cc, 0:64],
                                             rhs=BKl.t[:, hh, cc, 0:64], start=True, stop=True)
                            return r
                        P.op("pe", mmN, [BKl.d, ARl[ct].d], [bkN.d])
                        TTo("dve", Zp[0].t[:, half * 8:half * 8 + 8, :], bkN.t[0:64, :].rearrange("p (b c) -> p b c", c=64),
                            slm.unsqueeze(1).to_broadcast([64, 8, 64]), ALU.mult, [bkN.d, cst.d], [Zp[0].d])
                    chk(7)
                    Tm = TTm[ct]
                    TTo("pool", Tm[:], Yp[0][:], id64.unsqueeze(1).to_broadcast([64, 16, 64]), ALU.add, [Yp[0].d, cst.d], [Tm.d])
                    cur = 0
                    for step in range(5):
                        Yc, Zc, Yn, Zn = Yp[cur], Zp[cur], Yp[1 - cur], Zp[1 - cur]

                        def mm16(e, bankA, bankB, L, Rr):
                            r = None
                            for blk in range(16):
                                bank = bankA if blk < 8 else bankB
                                r = e.matmul(bank.t[0:64, (blk % 8) * 64:(blk % 8 + 1) * 64], lhsT=L.t[:, blk, :], rhs=Rr.t[:, blk, :],
                                             start=True, stop=True)
                            return r

                        def evac16(eng, dst, bankA, bankB, add=None):
                            for hf, bank in enumerate((bankA, bankB)):
                                src3 = bank.t[0:64, :].rearrange("p (b c) -> p b c", c=64)
                                dsl = dst.t[:, hf * 8:hf * 8 + 8, :]
                                if add is None:
                                    if eng == "act":
                                        P.op("act", lambda e, src3=src3, dsl=dsl: e.activation(out=dsl, in_=src3, func=AF.Copy), [bank.d], [dst.d])
                                    else:
                                        P.op("dve", lambda e, src3=src3, dsl=dsl: e.tensor_copy(out=dsl, in_=src3), [bank.d], [dst.d])
                                else:
                                    TTo("dve", dsl, src3, dsl, ALU.add, [bank.d, dst.d], [dst.d])
                        zA, zB = ps[4], ps[5]
                        P.op("pe", lambda e: mm16(e, zA, zB, Yc, Zc), [Yc.d, Zc.d], [zA.d, zB.d])
                        evac16("act", Zn, zA, zB)
                        if step < 4:
                            yA, yB = ps[6], ps[7]
                            P.op("pe", lambda e: mm16(e, yA, yB, Zc, Yc), [Yc.d, Zc.d], [yA.d, yB.d])
                            evac16("dve", Yn, yA, yB)
                        tA, tB = ps[2], ps[3]
                        P.op("pe", lambda e: mm16(e, tA, tB, Zn, Tm), [Zn.d, Tm.d], [tA.d, tB.d])
                        evac16("dve", Tm, tA, tB, add=True)
                        cur = 1 - cur

                d1(0)
                for ct in range(4):
                    d2(ct)
                    if ct < 3:
                        d1(ct + 1)
                    d3(ct)
                chk(8)
                for hh in range(2):
                    P.dma(PCl.t[:].rearrange("p (c h) k -> p c h k", h=2)[:, :, hh, :], PCv.t[64 * hh:64 * hh + 64, :, :],
                          [PCv.d], [PCl.d], PCl.d)
                ybank = [ps[4], ps[5], ps[6], ps[7]]
                for cc in range(8):
                    MSb = MSbs[cc % 2]
                    MSbn = MSbs[(cc + 1) % 2]
                    rb = zb_.next()

                    def mmR(e, cc=cc, rb=rb):
                        r = None
                        for ct in range(4):
                            for hh in range(2):
                                o = rb.t[0:64, (ct * 2 + hh) * 64:(ct * 2 + hh + 1) * 64]
                                e.matmul(o, lhsT=ARl[ct].t[:, hh, cc, 0:64], rhs=MSb.t[:, ct * 2 + hh, :],
                                         start=True, stop=False)
                                r = e.matmul(o, lhsT=AakT[ct].t[:, cc * 2 + hh, :], rhs=Vt[ct].t[0:64, cc, hh * 64:(hh + 1) * 64],
                                             start=False, stop=True)
                        return r
                    P.op("pe", mmR, [MSb.d] + [x_.d for c in range(4) for x_ in (ARl[c], AakT[c], Vt[c])], [rb.d])
                    P.op("act", lambda e, rb=rb: e.activation(out=R1[:], in_=rb.t[0:64, :], func=AF.Copy), [rb.d], [R1.d])
                    ub = zb_.next()

                    def mmU(e, cc=cc, ub=ub):
                        r = None
                        for ct in range(4):
                            for hh in range(2):
                                blk = ct * 2 + hh
                                r = e.matmul(ub.t[0:64, blk * 64:(blk + 1) * 64], lhsT=TTm[ct].t[:, cc * 2 + hh, :],
                                             rhs=R1.t[:, blk * 64:(blk + 1) * 64], start=True, stop=True)
                        return r
                    P.op("pe", mmU, [R1.d] + [TTm[c].d for c in range(4)], [ub.d])
                    P.op("dve", lambda e, ub=ub: e.tensor_copy(out=Ub[:], in_=ub.t[0:64, :]), [ub.d], [Ub.d])
                    mb = zb_.next()

                    def mmM(e, cc=cc, mb=mb):
                        r = None
                        for ct in range(4):
                            for hh in range(2):
                                blk = ct * 2 + hh
                                o2 = mb.t[0:64, blk * 64:(blk + 1) * 64]
                                e.matmul(o2, lhsT=Bh[ct].t[0:64, cc, hh * 64:(hh + 1) * 64], rhs=Ub.t[:, blk * 64:(blk + 1) * 64],
                                         start=True, stop=False, skip_group_check=True)
                                r = e.matmul(o2, lhsT=Kh[ct].t[0:64, cc, hh * 64:(hh + 1) * 64], rhs=Vt[ct].t[0:64, cc, hh * 64:(hh + 1) * 64],
                                             start=False, stop=True, skip_group_check=True)
                        return r
                    P.op("pe", mmM, [Ub.d] + [x_.d for c in range(4) for x_ in (Vt[c], Bh[c], Kh[c])], [mb.d])

                    def mmY(e, cc=cc, MSb=MSb):
                        r = None
                        for ct in range(4):
                            for hh in range(2):
                                blk = ct * 2 + hh
                                o = ybank[ct].t[64 * hh:64 * hh + 64, cc * 64:(cc + 1) * 64]
                                e.matmul(o, lhsT=MSb.t[:, blk, :], rhs=ARl[ct].t[:, hh, cc, 64:128],
                                         start=True, stop=False, skip_group_check=True)
                                e.matmul(o, lhsT=Ub.t[:, blk * 64:(blk + 1) * 64], rhs=ArbT[ct].t[:, cc * 2 + hh, :],
                                         start=False, stop=False, skip_group_check=True)
                                r = e.matmul(o, lhsT=Vt[ct].t[0:64, cc, hh * 64:(hh + 1) * 64], rhs=ArkT[ct].t[:, cc * 2 + hh, :],
                                             start=False, stop=True, skip_group_check=True)
                        return r
                    P.op("pe", mmY, [MSb.d, Ub.d] + [x_.d for c in range(4) for x_ in (ARl[c], ArbT[c], ArkT[c], Vt[c])],
                         [b_.d for b_ in ybank])
                    TTo("dve", MS[:], MS[:], PCl.t[:, :, cc:cc + 1].to_broadcast([64, 8, 64]), ALU.mult, [MS.d, PCl.d], [MS.d])
                    TTo("dve", MS[:], MS[:], mb.t[0:64, :].rearrange("p (a b) -> p a b", b=64), ALU.add, [MS.d, mb.d], [MS.d])
                    P.op("act", lambda e, MSbn=MSbn: e.activation(out=MSbn[:], in_=MS[:], func=AF.Copy), [MS.d], [MSbn.d])
                chk(9)
                for ct in range(4):
                    yb = ybank[ct]
                    ybf = bt_.next()
                    P.op("act", lambda e: e.activation(out=ysb[:], in_=yb[:], func=AF.Copy), [yb.d], [ysb.d])
                    P.op("pool", lambda e: e.tensor_copy(out=ybf[:], in_=ysb[:]), [ysb.d], [ybf.d])
                    mbk = zb_.next()
                    P.op("pe", lambda e: e.matmul(mbk[:], lhsT=blk64, rhs=ybf[:], start=True, stop=True), [ybf.d, cst.d], [mbk.d])
                    yc = tt_[0]
                    STTo(yc[:], mbk[:], -1.0 / 64, ysb[:], ALU.mult, ALU.add, [mbk.d, ysb.d], [yc.d])
                    sq = bt_.next()
                    ACTo(sq[:], yc[:], AF.Square, [yc.d], [sq.d])
                    vbk = zb_.next()
                    P.op("pe", lambda e: e.matmul(vbk[:], lhsT=blk64, rhs=sq[:], start=True, stop=True), [sq.d, cst.d], [vbk.d])
                    rs = tt_[1]
                    ACTo(rs[:], vbk[:], AF.Ln, [vbk.d, epsd.d], [rs.d], scale=1.0 / 64, bias=epsd.t[:, 0:1])
                    ACTo(rs[:], rs[:], AF.Exp, [rs.d], [rs.d], scale=-0.5)
                    TTo("dve", yc[:], yc[:], rs[:], ALU.mult, [yc.d, rs.d], [yc.d])
                    TSo("dve", yc[:], yc[:], p.t[:, PT_X + 4 + ct:PT_X + 5 + ct], p.t[:, PT_LNB + ct:PT_LNB + ct + 1], ALU.mult, ALU.add,
                        [yc.d, p.d], [yc.d])
                    TTo("pool", yc[:], yc[:], bon[ct][:], ALU.add, [yc.d, bon[ct].d], [yc.d])
                    TTo("pool", MTd[:], yc[:], gtd[ct][:], ALU.mult, [yc.d, gtd[ct].d], [MTd.d])
                    P.dma(mixed[12 + ct][:, tb * 512:(tb + 1) * 512], MTd[:], [MTd.d], [mixed_d[12 + ct]], MTd.d)

            try:
                for tb in range(NTB):
                    seg_body(tb)
            except _Stop:
                pass
            cast_eng[0] = "pool"
            P.barrier()

    res = dram("res", [S, D_MODEL], kind="Internal")

    def phase_out(l, src, dst):
        with contextlib.ExitStack() as st:
            wo = sbx(st, "wo", [128, 16, 1024], BF16)
            wo_d = [Dep() for _ in range(16)]
            wos = [sbx(st, f"wos{i}", [128, 1024]) for i in range(2)]
            for t in range(16):
                sl = wos[t % 2]
                P.dma(sl[:], w_out[l][t * 128:(t + 1) * 128, :], [], [sl.d], sl.d)
                P.op("pool", lambda e, sl=sl, t=t: e.tensor_copy(out=wo.t[:, t, :], in_=sl[:]), [sl.d], [wo_d[t]])
            mx = [sbx(st, f"mx{i}", [128, 16, 128], BF16) for i in range(2)]
            xr = [sbx(st, f"xr{i}", [128, 1024]) for i in range(2)]
            ot_ = [sbx(st, f"ot{i}", [128, 1024]) for i in range(2)]
            obanks = [(ps[0], ps[1]), (ps[2], ps[3])]
            for tt in range(NTT):
                m_ = mx[tt % 2]
                P.dma(m_[:], mixed[:, :, tt * 128:(tt + 1) * 128].rearrange("t p n -> p t n"), mixed_d, [m_.d], m_.d, q="act")
                x_ = xr[tt % 2]
                P.dma(x_[:], src[tt * 128:(tt + 1) * 128, :], [], [x_.d], x_.d, q="pool")
                o_ = ot_[tt % 2]
                for half in range(2):
                    bank = obanks[tt % 2][half]

                    def mm(e, m_=m_, half=half, bank=bank):
                        r = None
                        for t in range(16):
                            r = e.matmul(bank[:], lhsT=m_.t[:, t, :], rhs=wo.t[:, t, half * 512:(half + 1) * 512],
                                         start=(t == 0), stop=(t == 15))
                        return r
                    P.op("pe", mm, [m_.d] + wo_d, [bank.d])
                    P.op("dve", lambda e, o_=o_, x_=x_, bank=bank, half=half: e.tensor_tensor(
                        out=o_.t[:, half * 512:(half + 1) * 512], in0=bank[:], in1=x_.t[:, half * 512:(half + 1) * 512], op=ALU.add),
                        [bank.d, x_.d], [o_.d])
                P.dma(dst[tt * 128:(tt + 1) * 128, :], o_[:], [o_.d], [], o_.d)
            P.barrier()

    for l in range(depth):
        layer_prep(l)
        src = x_in if l == 0 else res
        phase_norm(l, src, lambda tt: [])
        if any(g in groups for g in "ABC"):
            phase_attn(l, [g for g in groups if g in "ABC"])
        if "D" in groups:
            phase_rwkv(l)
        if debug == "mixed":
            break
        if do_out:
            phase_out(l, src, out if l == depth - 1 else res)

    if debug == "mixed":
        with contextlib.ExitStack() as st:
            bounce = sbx(st, "bounce", [128, S], BF16)
            for mt in range(16):
                P.dma(bounce[:], mixed[mt], [mixed_d[mt]], [bounce.d], bounce.d)
                P.dma(dbg[mt], bounce[:], [bounce.d], [], bounce.d)
            P.barrier()
    return nc, es, P


_CACHE = {}


def kernel(**inputs):
    inp = {k: np.asarray(v) for k, v in inputs.items()}
    B, S, _ = inp["x"].shape
    depth = inp["w_in"].shape[0]
    nc, es, P = build(S=S, depth=depth)
    common = host_inputs(inp, depth)
    in_maps = []
    for b in range(B):
        m = dict(common)
        m["x"] = np.ascontiguousarray(inp["x"][b], dtype=np.float32)
        in_maps.append(m)
    res = run_bass_kernel_spmd(nc, in_maps, core_ids=list(range(B)))
    return np.stack([np.asarray(r["out"]) for r in res.results], axis=0).astype(np.float32)
```
